# Optimizing a Trainium2 kernel written in Bass

```python
import jax, jax.numpy as jnp
from jax import lax
import numpy as np

D_MODEL = 1024
BATCH = 8
SEQ = 4096
DEPTH = 2

MEM_LEN = 256
HEAD_DIM = 64
D_CONV = 256
CONV_K = 3
N_SWA_HEADS = 8
N_SWA_KV = 2
WINDOW = 128
BLOCK = 128
N_MEM_HEADS = 4
D_SWA = N_SWA_HEADS * HEAD_DIM
D_KV = N_SWA_KV * HEAD_DIM
D_MEMQ = N_MEM_HEADS * HEAD_DIM
D_MIX = D_CONV + D_SWA + D_MEMQ
D_IN = 3 * D_CONV + D_SWA + 2 * D_KV + D_MEMQ
D_FF = 2816
EPS = 1e-6

kernel_name = "hymba_style_conv_swa_memory_macaron"


def _split_points():
    widths = [D_CONV, D_CONV, D_CONV, D_SWA, D_KV, D_KV]
    return [int(v) for v in np.cumsum(widths)]


def _rms(x):
    xf = x.astype(jnp.float32)
    return (xf * lax.rsqrt(jnp.mean(xf * xf, axis=-1, keepdims=True) + EPS)).astype(x.dtype)


def rmsnorm(x, g):
    xf = x.astype(jnp.float32)
    y = xf * lax.rsqrt(jnp.mean(xf * xf, axis=-1, keepdims=True) + EPS)
    return (y * g.astype(jnp.float32)).astype(x.dtype)


def swiglu(x, w_up, w_down):
    gate, up = jnp.split(x @ w_up, 2, axis=-1)
    return (jax.nn.silu(gate) * up) @ w_down


def alibi_slopes(n):
    return jnp.asarray([2.0 ** (-8.0 * (i + 1) / n) for i in range(n)], dtype=jnp.float32)


def short_conv_mixer(b_gate, c_gate, u, conv_w):
    seq = u.shape[1]
    v = c_gate * u
    vp = jnp.pad(v, ((0, 0), (CONV_K - 1, 0), (0, 0)))
    y = conv_w[0] * vp[:, 0:seq]
    for j in range(1, CONV_K):
        y = y + conv_w[j] * vp[:, j:j + seq]
    return b_gate * y


def sliding_window_attention(q, k, v, sinks, slopes):
    bsz, seq = q.shape[0], q.shape[1]
    nb = seq // BLOCK
    grp = N_SWA_HEADS // N_SWA_KV
    qb = q.reshape(bsz, nb, BLOCK, N_SWA_KV, grp, HEAD_DIM)

    def band(t):
        tb = t.reshape(bsz, nb, BLOCK, N_SWA_KV, HEAD_DIM)
        prev = jnp.pad(tb[:, :-1], ((0, 0), (1, 0), (0, 0), (0, 0), (0, 0)))
        return jnp.concatenate([prev, tb], axis=2)

    kb, vb = band(k), band(v)
    scores = jnp.einsum('bnqkgd,bnskd->bnkgqs', qb, kb).astype(jnp.float32) * (HEAD_DIM ** -0.5)
    qi = jnp.arange(BLOCK)[:, None]
    sj = jnp.arange(2 * BLOCK)[None, :]
    dist = qi + BLOCK - sj
    blk = jnp.arange(nb)[:, None, None]
    valid = ((dist >= 0) & (dist < WINDOW))[None] & ((blk > 0) | (sj[None] >= BLOCK))
    bias = -slopes.reshape(N_SWA_KV, grp)[:, :, None, None] * dist.astype(jnp.float32)
    scores = jnp.where(valid[None, :, None, None], scores + bias, -jnp.inf)
    sink = jnp.broadcast_to(sinks.astype(jnp.float32).reshape(N_SWA_KV, grp)[None, None, :, :, None, None],
                            scores.shape[:-1] + (1,))
    probs = jax.nn.softmax(jnp.concatenate([scores, sink], axis=-1), axis=-1)[..., :-1]
    out = jnp.einsum('bnkgqs,bnskd->bnqkgd', probs.astype(v.dtype), vb)
    return out.reshape(bsz, seq, D_SWA)


def memory_cross_attention(q, mk, mv):
    s = jnp.einsum('bqhd,bmhd->bhqm', q, mk).astype(jnp.float32) * (HEAD_DIM ** -0.5)
    p = jax.nn.softmax(s, axis=-1)
    out = jnp.einsum('bhqm,bmhd->bqhd', p.astype(mv.dtype), mv)
    return out.reshape(q.shape[0], q.shape[1], D_MEMQ)


def setup_inputs(seed: int = 0) -> dict:
    key = jax.random.key(seed)
    ks = jax.random.split(key, 20)
    f32 = jnp.float32

    def nrm(k, shape, scale):
        return jax.random.normal(k, shape, f32) * scale

    def gain(k, shape):
        return 1.0 + 0.02 * jax.random.normal(k, shape, f32)

    return {
        "x": jax.random.normal(ks[0], (BATCH, SEQ, D_MODEL), f32),
        "mem": jax.random.normal(ks[1], (BATCH, MEM_LEN, D_MODEL), f32),
        "g_ffn1": gain(ks[2], (DEPTH, D_MODEL)),
        "w_ffn1_up": nrm(ks[3], (DEPTH, D_MODEL, 2 * D_FF), D_MODEL ** -0.5),
        "w_ffn1_down": nrm(ks[4], (DEPTH, D_FF, D_MODEL), D_FF ** -0.5),
        "g_mix": gain(ks[5], (DEPTH, D_MODEL)),
        "w_in": nrm(ks[6], (DEPTH, D_MODEL, D_IN), D_MODEL ** -0.5),
        "conv_w": nrm(ks[7], (DEPTH, CONV_K, D_CONV), CONV_K ** -0.5),
        "sinks": nrm(ks[8], (DEPTH, N_SWA_HEADS), 1.0),
        "g_mem": gain(ks[9], (DEPTH, D_MODEL)),
        "w_mem_kv": nrm(ks[10], (DEPTH, D_MODEL, 2 * D_MEMQ), D_MODEL ** -0.5),
        "g_grp": gain(ks[11], (DEPTH, D_MIX)),
        "w_out": nrm(ks[12], (DEPTH, D_MIX, D_MODEL), D_MIX ** -0.5),
        "g_ffn2": gain(ks[13], (DEPTH, D_MODEL)),
        "w_ffn2_up": nrm(ks[14], (DEPTH, D_MODEL, 2 * D_FF), D_MODEL ** -0.5),
        "w_ffn2_down": nrm(ks[15], (DEPTH, D_FF, D_MODEL), D_FF ** -0.5),
        "g_final": gain(ks[16], (D_MODEL,)),
    }


def reference(x, mem, g_ffn1, w_ffn1_up, w_ffn1_down, g_mix, w_in, conv_w, sinks, g_mem,
              w_mem_kv, g_grp, w_out, g_ffn2, w_ffn2_up, w_ffn2_down, g_final):
    bsz, seq = x.shape[0], x.shape[1]
    mem_len = mem.shape[1]
    slopes = alibi_slopes(N_SWA_HEADS)
    cuts = _split_points()
    h = x
    for l in range(DEPTH):
        h = h + 0.5 * swiglu(rmsnorm(h, g_ffn1[l]), w_ffn1_up[l], w_ffn1_down[l])

        n = rmsnorm(h, g_mix[l])
        proj = n @ w_in[l]
        b_g, c_g, u, q, k, v, qm = jnp.split(proj, cuts, axis=-1)

        y_conv = short_conv_mixer(b_g, c_g, u, conv_w[l])

        y_swa = sliding_window_attention(
            q.reshape(bsz, seq, N_SWA_HEADS, HEAD_DIM),
            k.reshape(bsz, seq, N_SWA_KV, HEAD_DIM),
            v.reshape(bsz, seq, N_SWA_KV, HEAD_DIM),
            sinks[l], slopes)

        mkv = rmsnorm(mem, g_mem[l]) @ w_mem_kv[l]
        mk, mv = jnp.split(mkv, 2, axis=-1)
        y_mem = memory_cross_attention(
            qm.reshape(bsz, seq, N_MEM_HEADS, HEAD_DIM),
            mk.reshape(bsz, mem_len, N_MEM_HEADS, HEAD_DIM),
            mv.reshape(bsz, mem_len, N_MEM_HEADS, HEAD_DIM))

        mixed = jnp.concatenate([_rms(y_conv), _rms(y_swa), _rms(y_mem)], axis=-1) * g_grp[l]
        h = h + mixed @ w_out[l]

        h = h + 0.5 * swiglu(rmsnorm(h, g_ffn2[l]), w_ffn2_up[l], w_ffn2_down[l])
    return rmsnorm(h, g_final)
```

```python
import contextlib

import numpy as np
import concourse.bass as bass
import concourse.mybir as mybir
from concourse.bass_utils import run_bass_kernel_spmd

F32 = mybir.dt.float32
BF16 = mybir.dt.bfloat16
AF = mybir.ActivationFunctionType
ALU = mybir.AluOpType

PE, ACT, DVE, POOL, SP = "pe", "act", "dve", "pool", "sp"
GRAN = 256

D = 1024
SEQ = 4096
NT = 1024
NTILES = SEQ // NT
SUB = 512
NSUB = NT // SUB
NBLK = NT // 128
DFF = 2816
NFC = DFF // 128
NG = NFC // 2
DEPTH = 2
MEM = 256
EPS = 1e-6
MASKV = -240000.0

G_FFN1, G_MIX, G_MEM, G_GRP, G_FFN2, G_FINAL = 0, 2, 4, 6, 8, 10
MODE = {"mixer": "interleave", "c_split": (4, 2), "bank": "sim", "kouter": True, "kouter_wo": 4, "sq_dve": 4, "load_act": 2, "store_act": 2, "store_inplace": True, "half_den": True, "b_hold": 3, "early_fc": 12, "late_fin": False, "dconv_at": 6, "load_pipe": True, "pa_dve": (1, 3, 5), "early_a0": 5, "etab": False, "v_nodup": True}


class _Op:
    __slots__ = ("eng", "fn", "deps", "is_dma", "dsem", "dval", "ord", "signal", "id", "t1", "t0", "crit", "tag")

    def __init__(self, eng, fn, is_dma):
        self.eng = eng
        self.fn = fn
        self.deps = {}
        self.is_dma = is_dma
        self.dsem = None
        self.dval = 0
        self.ord = 0
        self.signal = False
        self.id = -1
        self.t1 = 0.0


def _tracked(ap):
    return type(ap.tensor).__name__ in ("SBTensorHandle", "PSumTensorHandle")


def _region(ap):
    pat = ap.ap
    pstride = pat[0][0]
    off = ap.offset
    foff = off % pstride if pstride > 0 else off
    span = 1
    for st, cnt in pat[1:]:
        span += abs(st) * (cnt - 1)
    ds = mybir.dt.size(ap.dtype)
    b0, b1 = foff * ds, (foff + span) * ds
    if type(ap.tensor).__name__ == "PSumTensorHandle":
        b0 = (b0 // 2048) * 2048
        b1 = ((b1 + 2047) // 2048) * 2048
    return ap.name, b0, b1


def _nfree(ap):
    n = 1
    for _, cnt in ap.ap[1:]:
        n *= cnt
    return n


def _nbytes(ap):
    n = 1
    for _, cnt in ap.ap:
        n *= cnt
    return n * mybir.dt.size(ap.dtype)


class Prog:
    NDMA_SEMS = 14
    LAT = 0.15
    DMA_BPU = 300e3

    def __init__(self, nc):
        self.nc = nc
        self.ops = []
        self.last_w = {}
        self.readers = {}
        self.clock = {}
        self.unread = set()
        self.tag = ""
        self.bank_t = [0.0] * 8
        self.dma_free = 0.0

    def add(self, eng, fn, reads=(), writes=(), is_dma=False, cost=0.0):
        op = _Op(eng, fn, is_dma)
        op.id = len(self.ops)
        deps = op.deps
        if eng != PE:
            self._psum_check(reads)
        for ap in reads:
            if ap is None or not _tracked(ap):
                continue
            name, b0, b1 = _region(ap)
            for g in range(b0 // GRAN, (b1 - 1) // GRAN + 1):
                k = (name, g)
                w = self.last_w.get(k)
                if w is not None:
                    deps[w.id] = w
                self.readers.setdefault(k, []).append(op)
        for ap in writes:
            if ap is None or not _tracked(ap):
                continue
            name, b0, b1 = _region(ap)
            for g in range(b0 // GRAN, (b1 - 1) // GRAN + 1):
                k = (name, g)
                w = self.last_w.get(k)
                if w is not None and (w.eng != eng or w.is_dma or is_dma or eng != PE):
                    deps[w.id] = w
                for r in self.readers.get(k, ()):
                    if r is not op and (r.eng != eng or r.is_dma or is_dma or eng != PE):
                        deps[r.id] = r
                self.readers[k] = []
                self.last_w[k] = op
        deps.pop(op.id, None)
        t0 = self.clock.get(eng, 0.0)
        op.crit = None
        op.tag = self.tag
        for d in deps.values():
            if d.t1 + self.LAT > t0:
                t0 = d.t1 + self.LAT
                op.crit = d
        op.t0 = t0
        if is_dma:
            self.clock[eng] = t0 + (1.2 if eng == POOL else 0.06)
            ts = max(t0, self.dma_free)
            nb = max(_nbytes(ap) for ap in list(reads) + list(writes) if ap is not None)
            self.dma_free = ts + nb / self.DMA_BPU
            op.t1 = self.dma_free + 2.0
        else:
            op.t1 = t0 + cost
            self.clock[eng] = op.t1
        for ap in list(reads) + list(writes):
            if ap is not None and type(ap.tensor).__name__ == "PSumTensorHandle":
                _, b0, b1 = _region(ap)
                for b in range(b0 // 2048, (b1 - 1) // 2048 + 1):
                    self.bank_t[b] = op.t1 if self.bank_t[b] == float("inf") else max(self.bank_t[b], op.t1)
        self.ops.append(op)
        return op

    def emit(self, es):
        nc = self.nc
        engs = {PE: nc.tensor, ACT: nc.scalar, DVE: nc.vector, POOL: nc.gpsimd, SP: nc.sync}
        esem = {e: es.enter_context(nc.semaphore("sem_" + e)) for e in engs}
        nd = self.NDMA_SEMS
        dsems = [es.enter_context(nc.semaphore(f"sem_dma{i}")) for i in range(nd)]
        dcount = [0] * nd
        dlast = [None] * nd
        pools = {POOL: list(range(0, nd - 4)), SP: list(range(nd - 4, nd)), ACT: list(range(nd - 4, nd))}
        rrs = {POOL: 0, SP: 0, ACT: 0}

        seq = {}
        scount = {e: 0 for e in engs}
        for op in self.ops:
            if not op.is_dma:
                scount[op.eng] += 1
                seq[op.id] = scount[op.eng]
        wseq = {e: {} for e in engs}
        waited_dma = {e: set() for e in engs}
        ecount = {e: 0 for e in engs}
        plan = []
        for op in self.ops:
            waits = []
            if op.is_dma:
                pl = pools[op.eng]
                j = pl[rrs[op.eng] % len(pl)]
                rrs[op.eng] += 1
                prev = dlast[j]
                if prev is not None and prev.id not in waited_dma[op.eng]:
                    waits.append(("d", prev))
                    waited_dma[op.eng].add(prev.id)
                dcount[j] += 16
                op.dsem = j
                op.dval = dcount[j]
                dlast[j] = op
            best = {}
            for d in op.deps.values():
                if d.is_dma:
                    if d.id in waited_dma[op.eng]:
                        continue
                    waited_dma[op.eng].add(d.id)
                    waits.append(("d", d))
                else:
                    if d.eng == op.eng and op.eng == PE:
                        continue
                    b = best.get(d.eng)
                    if b is None or seq[d.id] > seq[b.id]:
                        best[d.eng] = d
            for pe_, d in best.items():
                if seq[d.id] > wseq[op.eng].get(pe_, 0):
                    wseq[op.eng][pe_] = seq[d.id]
                    waits.append(("e", d))
            plan.append(waits)
        need = set()
        for waits in plan:
            for kind, d in waits:
                if kind == "e":
                    need.add(d.id)
        for op in self.ops:
            if not op.is_dma and op.id in need:
                ecount[op.eng] += 1
                op.ord = ecount[op.eng]
                op.signal = True
        nwaits = 0
        for op, waits in zip(self.ops, plan):
            e = engs[op.eng]
            for kind, d in waits:
                if kind == "d":
                    e.wait_ge(dsems[d.dsem], d.dval)
                else:
                    e.wait_ge(esem[d.eng], d.ord)
                nwaits += 1
            ins = op.fn(e)
            if op.is_dma:
                ins.then_inc(dsems[op.dsem], 16)
            elif op.signal:
                ins.then_inc(esem[op.eng], 1)
        for j in range(nd):
            if dcount[j]:
                nc.sync.wait_ge(dsems[j], dcount[j])
        for en in engs:
            if ecount[en]:
                nc.sync.wait_ge(esem[en], ecount[en])
        self.stats = dict(est_us=max(o.t1 for o in self.ops), nops=len(self.ops), nwaits=nwaits, signals=dict(ecount), per_eng=dict(scount))

    @staticmethod
    def _c_act(ap):
        return 0.10 + 0.00098 * _nfree(ap)

    @staticmethod
    def _c_dve(out, *ins):
        n = _nfree(out)
        if all(mybir.dt.size(a.dtype) == 2 for a in (out,) + ins):
            return 0.08 + 0.0006 * n
        return 0.08 + 0.0012 * n

    def _raw_granules(self, ap):
        pat = ap.ap
        pstride = pat[0][0]
        foff = ap.offset % pstride if pstride > 0 else ap.offset
        span = 1
        for st, cnt in pat[1:]:
            span += abs(st) * (cnt - 1)
        ds = mybir.dt.size(ap.dtype)
        return range(foff * ds // GRAN, ((foff + span) * ds - 1) // GRAN + 1)

    def _psum_check(self, reads, fresh_write=None):
        for ap in reads:
            if ap is not None and type(ap.tensor).__name__ == "PSumTensorHandle":
                for g in self._raw_granules(ap):
                    self.unread.discard(g)
        if fresh_write is not None:
            for g in self._raw_granules(fresh_write):
                assert g not in self.unread, ("PSUM tile overwritten before it was read", g * GRAN // 2048)
                self.unread.add(g)

    def mm(self, out, lhsT, rhs, start=True, stop=True):
        self._psum_check([], out if start else None)
        n = _nfree(rhs)
        cost = max(n / 2400.0, 0.055) + 0.003
        return self.add(PE, lambda e: e.matmul(out, lhsT=lhsT, rhs=rhs, start=start, stop=stop),
                        reads=[lhsT, rhs], writes=[out], cost=cost)

    def transpose(self, out, in_, ident):
        self._psum_check([], out)
        return self.add(PE, lambda e: e.transpose(out, in_, ident), reads=[in_, ident], writes=[out], cost=0.12)

    def act(self, out, in_, func, bias=None, scale=None):
        kw = {}
        if bias is not None:
            kw["bias"] = bias
        if scale is not None:
            kw["scale"] = scale
        return self.add(ACT, lambda e: e.activation(out=out, in_=in_, func=func, **kw), reads=[in_], writes=[out],
                        cost=self._c_act(out))

    def copy(self, out, in_, eng=DVE):
        if eng == ACT:
            return self.add(ACT, lambda e: e.copy(out=out, in_=in_), reads=[in_], writes=[out], cost=self._c_act(out))
        return self.add(eng, lambda e: e.tensor_copy(out=out, in_=in_), reads=[in_], writes=[out],
                        cost=self._c_dve(out, in_))

    def tt(self, out, in0, in1, op, eng=DVE):
        return self.add(eng, lambda e: e.tensor_tensor(out=out, in0=in0, in1=in1, op=op),
                        reads=[in0, in1], writes=[out], cost=self._c_dve(out, in0, in1))

    def ts1(self, out, in0, s1, op0, eng=DVE):
        rd = [in0] + ([] if isinstance(s1, (int, float)) else [s1])
        return self.add(eng, lambda e: e.tensor_scalar(out=out, in0=in0, scalar1=s1, scalar2=None, op0=op0),
                        reads=rd, writes=[out], cost=self._c_dve(out, in0))

    def stt(self, out, in0, scalar, in1, op0, op1, eng=DVE):
        rd = [in0, in1] + ([] if isinstance(scalar, (int, float)) else [scalar])
        return self.add(eng, lambda e: e.scalar_tensor_tensor(out=out, in0=in0, scalar=scalar, in1=in1,
                                                              op0=op0, op1=op1),
                        reads=rd, writes=[out], cost=self._c_dve(out, in0, in1))

    def recip(self, out, in_):
        return self.add(DVE, lambda e: e.reciprocal(out=out, in_=in_), reads=[in_], writes=[out], cost=3.0)

    def memset(self, ap, val, eng=DVE):
        return self.add(eng, lambda e: e.memset(ap, val), reads=[], writes=[ap], cost=0.08 + 0.0006 * _nfree(ap))

    def dma(self, out, in_, eng=SP):
        return self.add(eng, lambda e: e.dma_start(out=out, in_=in_), reads=[in_], writes=[out], is_dma=True)


class WStream:
    def __init__(self, slots, loads):
        self.slots = slots
        self.loads = loads
        self.R = len(slots)
        self.issued = 0
        self.head = 0
        self.freed = 0
        self.other = None

    def _pump(self):
        lim = min(self.freed + self.R, len(self.loads))
        while self.issued < lim:
            i = self.issued
            self.loads[i][1](self.slots[i % self.R])
            self.issued += 1

    def acquire(self, key):
        i = self.head
        assert self.loads[i][0] == key, (self.loads[i][0], key)
        self._pump()
        assert self.issued > i
        self.head += 1
        return self.slots[i % self.R]

    def release(self, n=1):
        self.freed += n
        self._pump()
        if self.other is not None:
            self.other._pump()


def build_program(ntiles=NTILES, depth=DEPTH, raw_out=False, stop_after=None, dbg=()):
    nc = bass.Bass("TRN2", target_bir_lowering=False)
    P = Prog(nc)

    def din(name, shape):
        return nc.dram_tensor(name, list(shape), F32, kind="ExternalInput").ap()

    x_d = din("x", [SEQ, D])
    mem_d = din("mem", [MEM, D])
    wup_d = {1: din("w_ffn1_up", [DEPTH, D, 2 * DFF]), 2: din("w_ffn2_up", [DEPTH, D, 2 * DFF])}
    wdn_d = {1: din("w_ffn1_down", [DEPTH, DFF, D]), 2: din("w_ffn2_down", [DEPTH, DFF, D])}
    win_d = din("w_in", [DEPTH, D, 1792])
    wmem_d = din("w_mem_kv", [DEPTH, D, 512])
    wout_d = din("w_out", [DEPTH, D, D])
    gains_d = din("gains", [128, 88])
    convw_d = din("convw", [128, 12])
    sinks_d = din("sinks", [1, 16])
    ident_d = din("ident", [128, 128])
    biasc_d = din("bias_cur", [128, 2, 512])
    biasp_d = din("bias_prev", [128, 2, 512])
    out_d = nc.dram_tensor("out", [SEQ, D], F32, kind="ExternalOutput").ap()

    with contextlib.ExitStack() as es:
        def sb(name, shape, dt=F32):
            return es.enter_context(nc.sbuf_tensor(name, list(shape), dt))

        ps = es.enter_context(nc.psum_tensor("ps", [128, 8, 512], F32))
        bank_ctr = [0]
        reserved = set()

        pool_ctr = {}
        ATT = None
        LIN = None
        lin_pool = [None]

        def bank(pool=None):
            if pool is None and MODE["bank"] == "sim":
                now = P.clock.get(PE, 0.0)
                best = None
                for k in range(8):
                    b = (bank_ctr[0] + k) % 8
                    if b in reserved:
                        continue
                    key = max(P.bank_t[b], now)
                    if best is None or key < best[0] - 1e-9:
                        best = (key, b, k)
                bank_ctr[0] += best[2] + 1
                P.bank_t[best[1]] = float("inf")
                return best[1]
            while True:
                if pool is None:
                    b = bank_ctr[0] % 8
                    bank_ctr[0] += 1
                else:
                    name, cand = pool
                    i = pool_ctr.get(name, 0)
                    pool_ctr[name] = i + 1
                    b = cand[i % len(cand)]
                if b not in reserved:
                    return b

        def bank_reserve(pool=None):
            b = bank(pool)
            reserved.add(b)
            return b

        hT = sb("hT", [128, 8, NT])
        xn = sb("xn", [128, 8, NT], BF16)
        big = sb("big", [128, 11264])
        bigb = big[:].bitcast(BF16)
        hid = bigb.rearrange("p (c n) -> p c n", c=NFC)
        qT = bigb[:, 0:4096].rearrange("p (c n) -> p c n", c=4)
        kT = bigb[:, 4096:8192].rearrange("p (c n) -> p c n", c=4)
        qmT = bigb[:, 8192:10240].rearrange("p (c n) -> p c n", c=2)
        Vt = bigb[:, 10240:12288].rearrange("p (b n) -> p b n", b=8)
        ybuf = big[:, 6144:10240].rearrange("p (c n) -> p c n", c=8)
        onorm = big[:, 0:4096].rearrange("p (c n) -> p c n", c=8)
        ostage = big[:, 4096:6144].rearrange("p (b n) -> p b n", b=2)
        NXS = 8
        xbig = big[:, 6144:11264].rearrange("p (b n) -> p b n", b=5)
        xnf = xn[:].bitcast(F32).rearrange("p c n -> p (c n)").rearrange("p (b n) -> p b n", b=4)
        xstage_l = [xnf[:, i, :] for i in range(4)] + [xbig[:, i, :] for i in range(4)]
        mstage_l = [xbig[:, 4, :], ostage[:, 0, :]]
        memT = big[:, 0:2048].rearrange("p (c n) -> p c n", c=8)
        memn = bigb[:, 4096:6144].rearrange("p (c n) -> p c n", c=8)

        NWU, NWD = 4, 5
        wu_slots = [sb(f"wu{i}", [128, 8, 512], BF16) for i in range(NWU)]
        wd_slots = [sb(f"wd{i}", [128, NFC, 128], BF16) for i in range(NWD)]

        kcar = [sb(f"kcar{l}", [128, 4, 128], BF16) for l in range(DEPTH)]
        vcar = [sb(f"vcar{l}", [128, 256], BF16) for l in range(DEPTH)]
        ccar = [[sb(f"ccar{l}_{j}", [128, 2]) for j in range(2)] for l in range(DEPTH)]
        MKz = [sb(f"mkz{l}", [128, 4, MEM], BF16) for l in range(DEPTH)]
        MV = [sb(f"mv{l}", [128, 2, 256], BF16) for l in range(DEPTH)]
        biasC = sb("biasC", [128, 2, 512], BF16)
        biasP = sb("biasP", [128, 2, 512], BF16)
        SR = sb("SR", [128, 4, 512], BF16)
        identf = sb("identf", [128, 128])
        identb = sb("identb", [128, 128], BF16)
        ones_m = {n: sb(f"ones{n}", [128, 128], BF16) for n in (1024, 512, 256, 1)}
        sel = sb("sel", [128, 128], BF16)
        ones_h = [sb(f"ones_h{i}", [128, 128], BF16) for i in range(2)]
        sel_h = [sb(f"sel_h{i}", [128, 128], BF16) for i in range(2)]
        Gt = sb("Gt", [128, 88])
        CW = sb("CW", [128, 12])
        sk = sb("sk", [64, 16]); ske = sb("ske", [64, 16]); skh = sb("skh", [64, 16], BF16)
        skhf = sb("skhf", [64, 16]); skl = sb("skl", [64, 16])
        Pt = [[sb(f"P{i}_{j}", [128, 512], BF16) for j in range(2)] for i in range(2)]
        rec = [sb(f"rec{i}", [128, 256 if MODE["half_den"] else 512]) for i in range(2)]
        sg = [sb(f"sg{i}", [128, 512]) for i in range(2)]
        NSQ = 8
        sq = [sb(f"sq{i}", [128, 512], BF16) for i in range(NSQ)]
        sq_busy = [False] * NSQ
        sd = [sb(f"sd{i}", [128, 512]) for i in range(2)]
        lnd = sd
        rstd = [sb(f"rstd{i}", [128, 512]) for i in range(2)]
        vbuf = [sb(f"vbuf{j}", [128, 514]) for j in range(2)]
        csb = sb("csb", [128, 512])
        cacc = sb("cacc", [128, 512])

        ctr = {"sq": 0, "nrm": 0, "sg": 0, "ev": 0}

        def sq_get(n):
            for k in range(NSQ):
                i = (ctr["sq"] + k) % NSQ
                if not sq_busy[i]:
                    break
            else:
                raise AssertionError("no free square buffer")
            ctr["sq"] = i + 1
            sq_busy[i] = True
            return i, sq[i][:, 0:n]

        def evac(out, in_, act_of4=2):
            ctr["ev"] += 1
            pat = {0: (), 1: (1,), 2: (1, 3), 3: (0, 1, 3), 4: (0, 1, 2, 3)}[act_of4]
            P.copy(out, in_, eng=ACT if ctr["ev"] % 4 in pat else DVE)

        def v_k(w, l):
            return w[l].rearrange("(kc p) f -> p kc f", p=128)

        def ld_up(which, l, g):
            def f(slot):
                v = v_k(wup_d[which], l)
                P.dma(slot[:, :, 0:256], v[:, :, g * 256:(g + 1) * 256], eng=POOL)
                P.dma(slot[:, :, 256:512], v[:, :, DFF + g * 256:DFF + (g + 1) * 256], eng=POOL)
            return f

        def ld_in(l, i):
            def f(slot):
                v = v_k(win_d, l)
                if i == 0:
                    P.dma(slot[:, :, :], v[:, :, 0:512], eng=POOL)
                elif i == 1:
                    P.dma(slot[:, :, :], v[:, :, 512:1024], eng=POOL)
                else:
                    if i == 2:
                        P.dma(slot[:, :, 0:256], v[:, :, 1024:1280], eng=POOL)
                        base = 1280
                    else:
                        P.dma(slot[:, :, 0:256], v[:, :, 1536:1792], eng=POOL)
                        base = 1408
                    if i == 3 and MODE["v_nodup"]:
                        P.dma(slot[:, :, 256:384], v[:, :, base:base + 128], eng=POOL)
                    else:
                        for kv in range(2):
                            for h in range(2):
                                c0 = 256 + kv * 128 + h * 64
                                P.dma(slot[:, :, c0:c0 + 64], v[:, :, base + kv * 64:base + kv * 64 + 64], eng=POOL)
            return f

        def ld_out(l, i):
            def f(slot):
                P.dma(slot[:, :, :], v_k(wout_d, l)[:, :, i * 512:(i + 1) * 512], eng=POOL)
            return f

        def ld_mem(l):
            def f(slot):
                P.dma(slot[:, :, :], v_k(wmem_d, l)[:, :, :], eng=POOL)
            return f

        def ld_down(which, l, dc):
            def f(slot):
                P.dma(slot[:, :, :], v_k(wdn_d[which], l)[:, :, dc * 128:(dc + 1) * 128], eng=POOL)
            return f

        wu_loads = [(("mem", l), ld_mem(l)) for l in range(depth)]
        wd_loads = []
        for t in range(ntiles):
            for l in range(depth):
                wu_loads += [(("up", 1, l, g), ld_up(1, l, g)) for g in range(NG)]
                wd_loads += [(("down", 1, l, dc), ld_down(1, l, dc)) for dc in range(8)]
                wu_loads += [(("in", l, i), ld_in(l, i)) for i in range(4)]
                wu_loads += [(("out", l, i), ld_out(l, i)) for i in range(2)]
                wu_loads += [(("up", 2, l, g), ld_up(2, l, g)) for g in range(NG)]
                wd_loads += [(("down", 2, l, dc), ld_down(2, l, dc)) for dc in range(8)]
        WU = WStream(wu_slots, wu_loads)
        WD = WStream(wd_slots, wd_loads)

        def norm_split(srcs, gidx, gc0, nfeat, dsts, n):
            sqs = []
            for s_ in srcs:
                i, q = sq_get(n)
                P.act(q, s_, AF.Square)
                sqs.append((i, q))

            def part_b():
                i0 = ctr["nrm"] % 2
                ctr["nrm"] += 1
                b = bank(lin_pool[0])
                pss = ps[:, b, 0:n]
                last = len(sqs) - 1
                for k, (i, q) in enumerate(sqs):
                    P.mm(pss, ones_m[nfeat][:], q, start=(k == 0), stop=(k == last))
                    sq_busy[i] = False
                P.act(sd[i0][:, 0:n], pss, AF.Ln, bias=EPS, scale=1.0)
                P.act(rstd[i0][:, 0:n], sd[i0][:, 0:n], AF.Exp, scale=-0.5)
                for k, (s_, d_) in enumerate(zip(srcs, dsts)):
                    gcol = Gt[:, gidx * 8 + gc0 + k:gidx * 8 + gc0 + k + 1]
                    P.stt(d_, s_, gcol, rstd[i0][:, 0:n], ALU.mult, ALU.mult)
            return part_b

        def norm(srcs, gidx, gc0, nfeat, dsts, n):
            norm_split(srcs, gidx, gc0, nfeat, dsts, n)()

        class NormAcc:
            def __init__(self, s):
                self.s = s
                self.b = bank_reserve(lin_pool[0])
                self.cnt = 0
                self.half = None
                self.pending = []
                self.i0 = None

            def add(self, src, on_dve=False):
                i, q = sq_get(SUB)
                if on_dve:
                    P.tt(q, src, src, ALU.mult)
                else:
                    P.act(q, src, AF.Square)
                if self.half is None:
                    self.half = (i, q)
                else:
                    ia, qa = self.half
                    P.tt(qa, qa, q, ALU.add)
                    sq_busy[i] = False
                    self.half = None
                    self.pending.append((ia, qa))

            def flush(self):
                for i, q in self.pending:
                    P.mm(ps[:, self.b, :], ones_m[1024][:], q, start=(self.cnt == 0), stop=(self.cnt == 3))
                    sq_busy[i] = False
                    self.cnt += 1
                self.pending = []

            def finish_a(self):
                if self.i0 is not None:
                    return
                self.flush()
                assert self.cnt == 4 and self.half is None
                i0 = ctr["nrm"] % 2
                ctr["nrm"] += 1
                self.i0 = i0
                P.act(sd[i0][:], ps[:, self.b, :], AF.Ln, bias=EPS, scale=1.0)
                P.act(rstd[i0][:], sd[i0][:], AF.Exp, scale=-0.5)
                reserved.discard(self.b)

            def finish(self, gidx, dsts):
                self.finish_a()
                i0 = self.i0
                ssl = slice(self.s * SUB, (self.s + 1) * SUB)
                for c in range(8):
                    gcol = Gt[:, gidx * 8 + c:gidx * 8 + c + 1]
                    P.stt(dsts[c], hT[:, c, ssl], gcol, rstd[i0][:], ALU.mult, ALU.mult,
                          eng=(POOL if ("poolnorm" in dbg and c % 2 == 1) else DVE))

        def plain_accs():
            accs = [NormAcc(s) for s in range(NSUB)]
            for s in range(NSUB):
                for c in range(8):
                    accs[s].add(hT[:, c, s * SUB:(s + 1) * SUB])
                    accs[s].flush()
            return accs

        def ffn(which, l, accs, nxt=None):
            gidx = (G_FFN1 if which == 1 else G_FFN2) + l
            P.tag = "ffn-up"
            late = MODE["late_fin"] and MODE["kouter"]
            for s in range(NSUB):
                ssl = slice(s * SUB, (s + 1) * SUB)
                if s == 0 or not late:
                    accs[s].finish(gidx, [xn[:, c, ssl] for c in range(8)])
            for g in range(NG):
                w = WU.acquire(("up", which, l, g))
                for s in range(NSUB):
                    ssl = slice(s * SUB, (s + 1) * SUB)
                    if g == 0 and s == 1 and late:
                        accs[1].finish(gidx, [xn[:, c, ssl] for c in range(8)])
                    if g == 0 and MODE["kouter"]:
                        bks = [(bank(), bank()) for j in range(2)]
                        for k in range(8):
                            for j in range(2):
                                P.mm(ps[:, bks[j][0], :], w[:, k, j * 128:(j + 1) * 128], xn[:, k, ssl],
                                     start=(k == 0), stop=(k == 7))
                                P.mm(ps[:, bks[j][1], :], w[:, k, 256 + j * 128:256 + (j + 1) * 128], xn[:, k, ssl],
                                     start=(k == 0), stop=(k == 7))
                        for j in range(2):
                            sgb = sg[ctr["sg"] % 2]
                            ctr["sg"] += 1
                            P.act(sgb[:], ps[:, bks[j][0], :], AF.Silu)
                            P.tt(hid[:, 2 * g + j, ssl], sgb[:], ps[:, bks[j][1], :], ALU.mult)
                        continue
                    for j in range(2):
                        fc = 2 * g + j
                        bg, bu = bank(), bank()
                        for k in range(8):
                            P.mm(ps[:, bg, :], w[:, k, j * 128:(j + 1) * 128], xn[:, k, ssl], start=(k == 0), stop=(k == 7))
                        for k in range(8):
                            P.mm(ps[:, bu, :], w[:, k, 256 + j * 128:256 + (j + 1) * 128], xn[:, k, ssl],
                                 start=(k == 0), stop=(k == 7))
                        sgb = sg[ctr["sg"] % 2]
                        ctr["sg"] += 1
                        P.act(sgb[:], ps[:, bg, :], AF.Silu)
                        P.tt(hid[:, fc, ssl], sgb[:], ps[:, bu, :], ALU.mult)
                WU.release()
            nacc = [NormAcc(s) for s in range(NSUB)]
            pend = []
            P.tag = "ffn-down"
            for dc in range(8):
                w = WD.acquire(("down", which, l, dc))
                for s in range(NSUB):
                    ssl = slice(s * SUB, (s + 1) * SUB)
                    b = bank()
                    for fc in range(NFC):
                        P.mm(ps[:, b, :], w[:, fc, :], hid[:, fc, ssl], start=(fc == 0), stop=(fc == NFC - 1))
                        if (nxt is not None and dc == 7 and s == NSUB - 1 and MODE["early_fc"]
                                and fc == MODE["early_fc"] - 1):
                            nacc[0].finish_a()
                    if len(pend) >= 2:
                        nacc[pend.pop(0)].flush()
                    P.stt(hT[:, dc, ssl], ps[:, b, :], 0.5, hT[:, dc, ssl], ALU.mult, ALU.add)
                    nacc[s].add(hT[:, dc, ssl])
                    pend.append(s)
                WD.release()
            return nacc

        x_issued = {}

        def x_dma(t, upto):
            i = x_issued.get(t, 0)
            while i < min(upto, NBLK):
                P.dma(xstage_l[i % NXS], x_d[t * NT + i * 128:t * NT + (i + 1) * 128, :], eng=SP)
                i += 1
            x_issued[t] = i

        def load_x_accs(t):
            P.tag = "load"
            accs = []
            bps = NBLK // NSUB
            for b in range(NBLK):
                x_dma(t, b + NXS)
                st = xstage_l[b % NXS]
                for half in range(2):
                    bk = bank()
                    for cc in range(4):
                        c = half * 4 + cc
                        P.transpose(ps[:, bk, cc * 128:(cc + 1) * 128], st[:, c * 128:(c + 1) * 128], identf[:])
                    src = ps[:, bk, :].rearrange("p (c n) -> p c n", c=4)
                    evac(hT[:, half * 4:half * 4 + 4, b * 128:(b + 1) * 128], src, MODE["load_act"])
                if b % bps == bps - 1:
                    s = b // bps
                    acc = NormAcc(s)
                    for c in range(8):
                        acc.add(hT[:, c, s * SUB:(s + 1) * SUB], on_dve=(c in MODE["pa_dve"]))
                        acc.flush()
                    accs.append(acc)
            return accs

        def load_x(t):
            P.tag = "load"
            for b in range(NBLK):
                x_dma(t, b + NXS)
                st = xstage_l[b % NXS]
                for half in range(2):
                    bk = bank()
                    for cc in range(4):
                        c = half * 4 + cc
                        P.transpose(ps[:, bk, cc * 128:(cc + 1) * 128], st[:, c * 128:(c + 1) * 128], identf[:])
                    src = ps[:, bk, :].rearrange("p (c n) -> p c n", c=4)
                    evac(hT[:, half * 4:half * 4 + 4, b * 128:(b + 1) * 128], src, MODE["load_act"])

        ostage6 = big[:, 0:6144].rearrange("p (b n) -> p b n", b=6)

        def store_out_inplace(t, accs):
            P.tag = "store"
            for s in range(NSUB):
                ssl = slice(s * SUB, (s + 1) * SUB)
                accs[s].finish(G_FINAL, [hT[:, c, ssl] for c in range(8)])
            for s in range(NSUB):
                for bb in range(4):
                    i = s * 4 + bb
                    ost = ostage6[:, i % 6, :]
                    c0 = s * SUB + bb * 128
                    for half in range(2):
                        bk = bank()
                        for cc in range(4):
                            c = half * 4 + cc
                            P.transpose(ps[:, bk, cc * 128:(cc + 1) * 128], hT[:, c, c0:c0 + 128], identf[:])
                        evac(ost[:, half * 512:(half + 1) * 512], ps[:, bk, :], MODE["store_act"])
                    r0 = t * NT + c0
                    P.dma(out_d[r0:r0 + 128, :], ost, eng=SP)

        def store_out(t, accs):
            if MODE["store_inplace"] and not raw_out:
                return store_out_inplace(t, accs)
            P.tag = "store"
            for s in range(NSUB):
                ssl = slice(s * SUB, (s + 1) * SUB)
                if raw_out:
                    accs[s].flush()
                    reserved.discard(accs[s].b)
                    srcs = [hT[:, c, ssl] for c in range(8)]
                else:
                    accs[s].finish(G_FINAL, [onorm[:, c, :] for c in range(8)])
                    srcs = [onorm[:, c, :] for c in range(8)]
                for bb in range(4):
                    ost = ostage[:, bb % 2, :]
                    for half in range(2):
                        bk = bank()
                        for cc in range(4):
                            c = half * 4 + cc
                            P.transpose(ps[:, bk, cc * 128:(cc + 1) * 128], srcs[c][:, bb * 128:(bb + 1) * 128], identf[:])
                        evac(ost[:, half * 512:(half + 1) * 512], ps[:, bk, :], MODE["store_act"])
                    r0 = t * NT + s * SUB + bb * 128
                    P.dma(out_d[r0:r0 + 128, :], ost, eng=SP)

        def mixer(l, t, accs):
            P.tag = "mix-A"
            late = MODE["late_fin"] and MODE["kouter"] and MODE["mixer"] != "seq"
            for s in range(NSUB):
                ssl = slice(s * SUB, (s + 1) * SUB)
                if s == 0 or not late:
                    accs[s].finish(G_MIX + l, [xn[:, c, ssl] for c in range(8)])
            for kv in range(2):
                P.memset(kT[64:128, kv * 2 + 0, :], 0.0)
                P.memset(kT[0:64, kv * 2 + 1, :], 0.0)
            dconv = [None] * NSUB
            dgrp = [None] * NSUB
            w0 = WU.acquire(("in", l, 0))
            w1 = WU.acquire(("in", l, 1))
            w2 = WU.acquire(("in", l, 2))
            w3 = WU.acquire(("in", l, 3))

            def proj(out_bank_ap, wslot, c0, rhs_of_k, n=128):
                for k in range(8):
                    P.mm(out_bank_ap, wslot[:, k, c0:c0 + n], rhs_of_k(k), start=(k == 0), stop=(k == 7))

            def proj_gen(s):
                ssl = slice(s * SUB, (s + 1) * SUB)
                xs = lambda k: xn[:, k, ssl]
                last = (s == NSUB - 1)
                for j in range(2):
                    pC, pX, pB = bank(lin_pool[0]), bank(lin_pool[0]), bank(lin_pool[0])
                    reserved.update((pC, pX, pB))
                    if s == 0 and j == 0 and MODE["kouter"]:
                        for k in range(8):
                            P.mm(ps[:, pC, :], w0[:, k, 256:384], xs(k), start=(k == 0), stop=(k == 7))
                            P.mm(ps[:, pX, :], w1[:, k, 0:128], xs(k), start=(k == 0), stop=(k == 7))
                            P.mm(ps[:, pB, :], w0[:, k, 0:128], xs(k), start=(k == 0), stop=(k == 7))
                        yield
                        yield
                    else:
                        proj(ps[:, pC, :], w0, 256 + j * 128, xs)
                        yield
                        proj(ps[:, pX, :], w1, j * 128, xs)
                        yield
                        proj(ps[:, pB, :], w0, j * 128, xs)
                    if j == 0 and s > 0 and dconv[s - 1] is not None:
                        dconv[s - 1]()
                        dconv[s - 1] = None
                    vb = vbuf[j]
                    cw = CW[:, (l * 2 + j) * 3:(l * 2 + j) * 3 + 3]
                    P.copy(csb[:], ps[:, pC, :], eng=ACT)
                    P.copy(vb[:, 0:2], ccar[l][j][:], eng=DVE)
                    P.tt(vb[:, 2:514], csb[:], ps[:, pX, :], ALU.mult)
                    P.ts1(cacc[:], vb[:, 2:514], cw[:, 2:3], ALU.mult)
                    P.stt(cacc[:], vb[:, 1:513], cw[:, 1:2], cacc[:], ALU.mult, ALU.add)
                    P.stt(cacc[:], vb[:, 0:512], cw[:, 0:1], cacc[:], ALU.mult, ALU.add)
                    P.tt(ybuf[:, j, :], cacc[:], ps[:, pB, :], ALU.mult)
                    P.copy(ccar[l][j][:], vb[:, 512:514], eng=DVE)
                    for b_ in (pC, pX, pB):
                        reserved.discard(b_)
                    yield
                if last:
                    WU.release(1)
                for c in range(4):
                    b = bank(lin_pool[0])
                    if c < 2:
                        proj(ps[:, b, :], w1, 256 + c * 128, xs)
                    else:
                        proj(ps[:, b, :], w2, (c - 2) * 128, xs)
                    evac(qT[:, c, ssl], ps[:, b, :])
                    if c == 1 and last:
                        WU.release(1)
                    yield
                for kv in range(2):
                    b = bank(lin_pool[0])
                    proj(ps[:, b, :], w2, 256 + kv * 128, xs)
                    P.copy(kT[0:64, kv * 2 + 0, ssl], ps[0:64, b, :], eng=ACT)
                    P.copy(kT[64:128, kv * 2 + 1, ssl], ps[64:128, b, :], eng=DVE)
                    yield
                if last:
                    WU.release(1)
                for c in range(2):
                    b = bank(lin_pool[0])
                    proj(ps[:, b, :], w3, c * 128, xs)
                    evac(qmT[:, c, ssl], ps[:, b, :])
                    yield
                for bl in range(4):
                    blk = s * 4 + bl
                    b = bank(lin_pool[0])
                    if MODE["v_nodup"]:
                        for k in range(8):
                            P.mm(ps[:, b, 0:128], xn[:, k, blk * 128:(blk + 1) * 128], w3[:, k, 256:384],
                                 start=(k == 0), stop=(k == 7))
                        vsrc = ps[:, b, 0:128].rearrange("p (kv d) -> p kv d", kv=2).unsqueeze(2)
                        evac(Vt[:, blk, :].rearrange("p (kv h d) -> p kv h d", kv=2, h=2),
                             vsrc.to_broadcast([128, 2, 2, 64]))
                    else:
                        for k in range(8):
                            P.mm(ps[:, b, 0:256], xn[:, k, blk * 128:(blk + 1) * 128], w3[:, k, 256:512],
                                 start=(k == 0), stop=(k == 7))
                        evac(Vt[:, blk, :], ps[:, b, 0:256])
                    yield
                if last:
                    WU.release(1)
                dconv[s] = norm_split([ybuf[:, 0, :], ybuf[:, 1, :]], G_GRP + l, 0, 256,
                                      [xn[:, 0, ssl], xn[:, 1, ssl]], SUB)

            def kblk(v, blk):
                return kT[:, v, blk * 128:(blk + 1) * 128] if blk >= 0 else kcar[l][:, v, :]

            def vblk(kv, blk):
                return Vt[:, blk, kv * 128:(kv + 1) * 128] if blk >= 0 else vcar[l][:, kv * 128:(kv + 1) * 128]

            def swa1(u):
                blk, kv, pi = u["blk"], u["kv"], u["pi"]
                qsl = slice(blk * 128, (blk + 1) * 128)
                has_prev = (t * NBLK + blk) > 0
                u["has_prev"] = has_prev
                et = MODE["etab"]
                bc = bank(ATT)
                if not et:
                    P.mm(ps[:, bc, :], identb[:], biasC[:, kv, :], start=True, stop=False)
                for par in range(2):
                    P.mm(ps[:, bc, par * 256:(par + 1) * 256].rearrange("p (j q) -> p j q", j=2),
                         kblk(kv * 2 + par, blk), qT[:, kv * 2:kv * 2 + 2, qsl],
                         start=bool(et), stop=(True if et else par == 1))
                P.act(Pt[pi][0][:], ps[:, bc, :], AF.Exp, scale=0.125)
                if et:
                    P.tt(Pt[pi][0][:], Pt[pi][0][:], biasC[:, kv, :], ALU.mult)
                if has_prev:
                    bp = bank(ATT)
                    if not et:
                        P.mm(ps[:, bp, :], identb[:], biasP[:, kv, :], start=True, stop=False)
                    for par in range(2):
                        P.mm(ps[:, bp, par * 256:(par + 1) * 256].rearrange("p (j q) -> p j q", j=2),
                             kblk(kv * 2 + par, blk - 1), qT[:, kv * 2:kv * 2 + 2, qsl],
                             start=bool(et), stop=(True if et else par == 1))
                    P.act(Pt[pi][1][:], ps[:, bp, :], AF.Exp, scale=0.125)
                    if et:
                        P.tt(Pt[pi][1][:], Pt[pi][1][:], biasP[:, kv, :], ALU.mult)

            def finish(u, bo, bd, c0, grouped=False):
                pi = u["pi"]
                bl = u["blk"] % 4
                half = MODE["half_den"]
                nd = 256 if half else 512
                P.act(lnd[pi][:, 0:nd], ps[:, bd, 0:nd], AF.Ln)
                P.act(rec[pi][:, 0:nd], lnd[pi][:, 0:nd], AF.Exp, scale=-1.0)
                for par in range(2):
                    p0 = par * 64
                    if grouped:
                        src = ps[p0:p0 + 64, bo, par * 256:(par + 1) * 256].rearrange("p (g q) -> p g q", g=2)
                    else:
                        src = ps[p0:p0 + 64, bo, :].rearrange("p (g q) -> p g q", g=4)[:, par::2, :]
                    if half:
                        rv = rec[pi][p0:p0 + 64, 0:256].rearrange("p (g q) -> p g q", g=2)
                    elif grouped:
                        rv = rec[pi][p0:p0 + 64, par * 256:(par + 1) * 256].rearrange("p (g q) -> p g q", g=2)
                    else:
                        rv = rec[pi][p0:p0 + 64, :].rearrange("p (g q) -> p g q", g=4)[:, par::2, :]
                    dst = ybuf[p0:p0 + 64, c0:c0 + 2, bl * 128:(bl + 1) * 128]
                    P.tt(dst, src, rv, ALU.mult)

            def swa2(u):
                blk, kv, pi = u["blk"], u["kv"], u["pi"]
                hp = u["has_prev"]
                bo, bd = bank(ATT), bank(ATT)
                P.mm(ps[:, bo, :], vblk(kv, blk), Pt[pi][0][:], start=True, stop=not hp)
                if hp:
                    P.mm(ps[:, bo, :], vblk(kv, blk - 1), Pt[pi][1][:], start=False, stop=True)
                if MODE["half_den"]:
                    dn = ps[:, bd, 0:256]
                    first = True
                    for pt in ([Pt[pi][0]] + ([Pt[pi][1]] if hp else [])):
                        for h in range(2):
                            P.mm(dn, ones_h[h][:], pt[:, h * 256:(h + 1) * 256], start=first, stop=False)
                            first = False
                    for h in range(2):
                        P.mm(dn, sel_h[h][:], SR[:, l * 2 + kv, h * 256:(h + 1) * 256], start=False, stop=(h == 1))
                else:
                    P.mm(ps[:, bd, :], ones_m[1][:], Pt[pi][0][:], start=True, stop=False)
                    if hp:
                        P.mm(ps[:, bd, :], ones_m[1][:], Pt[pi][1][:], start=False, stop=False)
                    P.mm(ps[:, bd, :], sel[:], SR[:, l * 2 + kv, :], start=False, stop=True)
                finish(u, bo, bd, 2 + kv * 2, grouped=True)

            def mem1(u):
                blk, pi = u["blk"], u["pi"]
                qsl = slice(blk * 128, (blk + 1) * 128)
                for mc in range(2):
                    b = bank(ATT)
                    for hm in range(4):
                        P.mm(ps[:, b, hm * 128:(hm + 1) * 128], MKz[l][:, hm, mc * 128:(mc + 1) * 128],
                             qmT[:, hm // 2, qsl], start=True, stop=True)
                    P.act(Pt[pi][mc][:], ps[:, b, :], AF.Exp, scale=0.125)

            def mem2(u):
                pi = u["pi"]
                bo, bd = bank(ATT), bank(ATT)
                for c in range(2):
                    for mc in range(2):
                        P.mm(ps[:, bo, c * 256:(c + 1) * 256], MV[l][:, mc, c * 128:(c + 1) * 128],
                             Pt[pi][mc][:, c * 256:(c + 1) * 256], start=(mc == 0), stop=(mc == 1))
                if MODE["half_den"]:
                    dn = ps[:, bd, 0:256].rearrange("p (g q) -> p g q", g=2)
                    for mc in range(2):
                        pv = Pt[pi][mc][:].rearrange("p (g q) -> p g q", g=4)
                        for h in range(2):
                            P.mm(dn, ones_h[h][:], pv[:, h::2, :], start=(mc == 0 and h == 0), stop=(mc == 1 and h == 1))
                else:
                    for mc in range(2):
                        P.mm(ps[:, bd, :], ones_m[1][:], Pt[pi][mc][:], start=(mc == 0), stop=(mc == 1))
                finish(u, bo, bd, 6)

            def attn_gen(s, pre_hook):
                ssl = slice(s * SUB, (s + 1) * SUB)
                units = []
                for bl in range(4):
                    blk = s * 4 + bl
                    units.append(dict(kind="swa", blk=blk, kv=0))
                    units.append(dict(kind="swa", blk=blk, kv=1))
                    units.append(dict(kind="mem", blk=blk))
                prev = None
                for i, u in enumerate(units):
                    u["pi"] = i % 2
                    (swa1 if u["kind"] == "swa" else mem1)(u)
                    pre_hook(i)
                    if prev is not None:
                        (swa2 if prev["kind"] == "swa" else mem2)(prev)
                    prev = u
                    yield
                (swa2 if prev["kind"] == "swa" else mem2)(prev)
                nb1 = norm_split([ybuf[:, 2 + c, :] for c in range(4)], G_GRP + l, 2, 512,
                                 [xn[:, 2 + c, ssl] for c in range(4)], SUB)
                nb2 = norm_split([ybuf[:, 6 + c, :] for c in range(2)], G_GRP + l, 6, 256,
                                 [xn[:, 6 + c, ssl] for c in range(2)], SUB)
                dgrp[s] = (lambda a, b: (lambda: (a(), b())))(nb1, nb2)

            nacc = [None] * NSUB
            pend = []
            wo = []

            def wout_gen(s):
                ssl = slice(s * SUB, (s + 1) * SUB)
                nacc[s] = NormAcc(s)
                kob = []
                if s == 1 and MODE["kouter_wo"]:
                    nko = MODE["kouter_wo"]
                    kob = [bank(lin_pool[0]) for _ in range(nko)]
                    for k in range(8):
                        for dc in range(nko):
                            P.mm(ps[:, kob[dc], :], wo[dc // 4][:, k, (dc % 4) * 128:(dc % 4 + 1) * 128], xn[:, k, ssl],
                                 start=(k == 0), stop=(k == 7))
                for dc in range(8):
                    w = wo[dc // 4]
                    if dc < len(kob):
                        b = kob[dc]
                    else:
                        b = bank(lin_pool[0])
                        for k in range(8):
                            P.mm(ps[:, b, :], w[:, k, (dc % 4) * 128:(dc % 4 + 1) * 128], xn[:, k, ssl],
                                 start=(k == 0), stop=(k == 7))
                    if len(pend) >= 2:
                        nacc[pend.pop(0)].flush()
                    P.tt(hT[:, dc, ssl], ps[:, b, :], hT[:, dc, ssl], ALU.add)
                    nacc[s].add(hT[:, dc, ssl], on_dve=(s == 0 and dc < MODE["sq_dve"]))
                    pend.append(s)
                    if s == 1 and MODE["early_a0"] and dc == MODE["early_a0"] - 1:
                        nacc[0].finish_a()
                    yield

            def step(g):
                try:
                    next(g)
                    return True
                except StopIteration:
                    return False

            def merge(ga, gb, na, nb, b_start, b_hold, after0=None):
                done_b = 0
                slots = na - b_start
                for i in range(na):
                    assert step(ga)
                    if i == 0 and after0 is not None:
                        after0()
                    if i >= b_start:
                        target = ((i - b_start + 1) * (nb - b_hold)) // slots
                        while done_b < target:
                            assert step(gb)
                            done_b += 1
                assert not step(ga)
                return done_b

            def drain(g):
                while step(g):
                    pass

            def hook_none(i):
                pass

            def hook_prev(i):
                if i == 1:
                    dgrp[0]()
                    dgrp[0] = None
                if i == MODE["dconv_at"]:
                    dconv[1]()
                    dconv[1] = None

            assert NSUB == 2
            if MODE["mixer"] == "seq":
                def hook_c1(i):
                    if i == 1:
                        dconv[1]()
                        dconv[1] = None

                def hook_g0(i):
                    if i == 1:
                        dgrp[0]()
                        dgrp[0] = None
                drain(proj_gen(0))
                drain(proj_gen(1))
                drain(attn_gen(0, hook_c1))
                drain(attn_gen(1, hook_g0))
                P.copy(kcar[l][:, :, :], kT[:, :, (NBLK - 1) * 128:NBLK * 128], eng=DVE)
                P.copy(vcar[l][:, :], Vt[:, NBLK - 1, :], eng=DVE)
                wo.append(WU.acquire(("out", l, 0)))
                wo.append(WU.acquire(("out", l, 1)))
                drain(wout_gen(0))
                dgrp[1]()
                dgrp[1] = None
                drain(wout_gen(1))
            else:
                nC, hC = MODE["c_split"]
                g0 = proj_gen(0)
                if late:
                    step(g0)
                    accs[1].finish(G_MIX + l, [xn[:, c, SUB:2 * SUB] for c in range(8)])
                drain(g0)
                lin_pool[0] = LIN
                P.tag = "mix-B"
                gbB = proj_gen(1)
                merge(attn_gen(0, hook_none), gbB, 12, 18, 0, MODE["b_hold"])
                wo.append(WU.acquire(("out", l, 0)))
                wo.append(WU.acquire(("out", l, 1)))
                P.tag = "mix-C"
                gb = wout_gen(0)
                merge(attn_gen(1, hook_prev), gb, 12, 8, 2, nC, after0=lambda: drain(gbB))
                P.tag = "mix-D"
                P.copy(kcar[l][:, :, :], kT[:, :, (NBLK - 1) * 128:NBLK * 128], eng=DVE)
                P.copy(vcar[l][:, :], Vt[:, NBLK - 1, :], eng=DVE)
                for _ in range(hC):
                    step(gb)
                dgrp[1]()
                dgrp[1] = None
                drain(gb)
                lin_pool[0] = None
                drain(wout_gen(1))
            WU.release(2)
            return nacc

        P.dma(identf[:], ident_d, eng=SP)
        x_dma(0, NXS)
        P.dma(Gt[:], gains_d, eng=SP)
        P.dma(CW[:], convw_d, eng=SP)
        P.dma(biasC[:], biasc_d, eng=POOL)
        P.dma(biasP[:], biasp_d, eng=POOL)
        if MODE["etab"]:
            for kv in range(2):
                P.act(biasC[:, kv, :], biasC[:, kv, :], AF.Exp, scale=0.125)
                P.act(biasP[:, kv, :], biasP[:, kv, :], AF.Exp, scale=0.125)
        P.copy(identb[:], identf[:], eng=DVE)
        for n in (1024, 512, 256, 1):
            P.memset(ones_m[n][:], 1.0 / n)
        P.memset(sel[:], 0.0)
        P.memset(sel[0:1, :], 1.0)
        P.memset(sel[32:33, :], 1.0)
        for i in range(2):
            P.memset(ones_h[i][:], 0.0)
            P.memset(ones_h[i][:, i * 64:(i + 1) * 64], 1.0)
            P.memset(sel_h[i][:], 0.0)
            P.memset(sel_h[i][0:1, i * 64:(i + 1) * 64], 1.0)
            P.memset(sel_h[i][32:33, i * 64:(i + 1) * 64], 1.0)
        for l in range(DEPTH):
            P.memset(MKz[l][:], 0.0)
            for j in range(2):
                P.memset(ccar[l][j][:], 0.0)
        P.memset(sk[:], 0.0)
        P.dma(sk[0:1, :], sinks_d, eng=SP)
        P.dma(sk[32:33, :], sinks_d, eng=SP)
        P.act(ske[:], sk[:], AF.Exp)
        P.copy(skh[:], ske[:])
        P.copy(skhf[:], skh[:])
        P.tt(skl[:], ske[:], skhf[:], ALU.subtract)
        P.memset(SR[:], 0.0)
        for lk in range(4):
            for par in range(2):
                src_hi = skh[0:1, lk * 4 + par:lk * 4 + 4:2].unsqueeze(2).to_broadcast([1, 2, 128])
                src_lo = skl[32:33, lk * 4 + par:lk * 4 + 4:2].unsqueeze(2).to_broadcast([1, 2, 128])
                P.copy(SR[0:1, lk, par * 256:(par + 1) * 256].rearrange("p (g q) -> p g q", g=2), src_hi)
                P.copy(SR[32:33, lk, par * 256:(par + 1) * 256].rearrange("p (g q) -> p g q", g=2), src_lo)
        for mc in range(2):
            P.dma(mstage_l[mc], mem_d[mc * 128:(mc + 1) * 128, :], eng=SP)
        for mc in range(2):
            for half in range(2):
                bk = bank()
                for cc in range(4):
                    c = half * 4 + cc
                    P.transpose(ps[:, bk, cc * 128:(cc + 1) * 128], mstage_l[mc][:, c * 128:(c + 1) * 128], identf[:])
                evac(memT[:, half * 4:half * 4 + 4, mc * 128:(mc + 1) * 128],
                     ps[:, bk, :].rearrange("p (c n) -> p c n", c=4))
        for l in range(depth):
            wm = WU.acquire(("mem", l))
            norm([memT[:, c, :] for c in range(8)], G_MEM + l, 0, 1024, [memn[:, c, :] for c in range(8)], MEM)
            for c in range(2):
                b = bank()
                for k in range(8):
                    P.mm(ps[:, b, 0:256], wm[:, k, c * 128:(c + 1) * 128], memn[:, k, :], start=(k == 0), stop=(k == 7))
                P.copy(MKz[l][0:64, 2 * c, :], ps[0:64, b, 0:256], eng=ACT)
                P.copy(MKz[l][64:128, 2 * c + 1, :], ps[64:128, b, 0:256], eng=DVE)
            for mc in range(2):
                b = bank()
                for k in range(8):
                    P.mm(ps[:, b, 0:256], memn[:, k, mc * 128:(mc + 1) * 128], wm[:, k, 256:512],
                         start=(k == 0), stop=(k == 7))
                evac(MV[l][:, mc, :], ps[:, b, 0:256])
            WU.release()

        done = False
        WD._pump()
        for t in range(ntiles):
            if MODE["load_pipe"]:
                accs = load_x_accs(t)
            else:
                load_x(t)
                accs = plain_accs()
            for l in range(depth):
                dst_xn = lambda c, ssl: xn[:, c, ssl]
                dst_ht = lambda c, ssl: hT[:, c, ssl]
                accs = ffn(1, l, accs, nxt=(G_MIX + l, dst_xn))
                if stop_after == ("ffn1", l):
                    done = True
                    break
                accs = mixer(l, t, accs)
                if stop_after == ("mixer", l):
                    done = True
                    break
                if l + 1 < depth:
                    nxt2 = (G_FFN1 + l + 1, dst_xn)
                elif MODE["store_inplace"] and not raw_out:
                    nxt2 = (G_FINAL, dst_ht)
                else:
                    nxt2 = None
                accs = ffn(2, l, accs, nxt=nxt2)
            if t + 1 < ntiles and not done:
                x_dma(t + 1, NXS)
            store_out(t, accs)
            if done:
                break

        P.emit(es)
        build_program.stats = P.stats
        build_program.prog = P
    return nc


def host_layout(inputs):
    f32 = np.float32
    g = lambda k: np.asarray(inputs[k], dtype=f32)
    gl = [g("g_ffn1")[0], g("g_ffn1")[1], g("g_mix")[0], g("g_mix")[1], g("g_mem")[0], g("g_mem")[1],
          g("g_grp")[0], g("g_grp")[1], g("g_ffn2")[0], g("g_ffn2")[1], g("g_final")]
    gains = np.ascontiguousarray(np.stack(gl, 0).reshape(11, 8, 128).transpose(2, 0, 1).reshape(128, 88))
    cw = g("conv_w")
    convw = np.ascontiguousarray(cw.reshape(DEPTH, 3, 2, 128).transpose(3, 0, 2, 1).reshape(128, 12))
    sinks = np.ascontiguousarray(g("sinks").reshape(1, 16))
    ident = np.eye(128, dtype=f32)
    slopes = np.array([2.0 ** (-8.0 * (i + 1) / 8) for i in range(8)], dtype=np.float64)
    s_i = np.arange(128)[:, None]
    q_i = np.arange(128)[None, :]
    bc = np.zeros((128, 2, 4, 128), f32)
    bp = np.zeros((128, 2, 4, 128), f32)
    for kv in range(2):
        for gg in range(4):
            sl = slopes[kv * 4 + gg]
            pos = (gg % 2) * 2 + gg // 2
            bc[:, kv, pos, :] = np.where(q_i >= s_i, -8.0 * sl * (q_i - s_i), MASKV)
            bp[:, kv, pos, :] = np.where(s_i > q_i, -8.0 * sl * (q_i + 128 - s_i), MASKV)
    shared = dict(
        w_ffn1_up=g("w_ffn1_up"), w_ffn1_down=g("w_ffn1_down"), w_in=g("w_in"), w_mem_kv=g("w_mem_kv"),
        w_out=g("w_out"), w_ffn2_up=g("w_ffn2_up"), w_ffn2_down=g("w_ffn2_down"),
        gains=gains, convw=convw, sinks=sinks, ident=ident,
        bias_cur=np.ascontiguousarray(bc.reshape(128, 2, 512)), bias_prev=np.ascontiguousarray(bp.reshape(128, 2, 512)),
    )
    x = g("x")
    mem = g("mem")
    in_maps = []
    for i in range(x.shape[0]):
        m = dict(shared)
        m["x"] = np.ascontiguousarray(x[i])
        m["mem"] = np.ascontiguousarray(mem[i])
        in_maps.append(m)
    return in_maps


_NC_CACHE = {}


def kernel(**inputs):
    in_maps = host_layout(inputs)
    if "nc" not in _NC_CACHE:
        _NC_CACHE["nc"] = build_program()
    nc = _NC_CACHE["nc"]
    res = run_bass_kernel_spmd(nc, in_maps, core_ids=list(range(len(in_maps))))
    out = np.stack([np.asarray(r["out"], dtype=np.float32) for r in res.results], axis=0)
    return out
```

```python
import contextlib

import numpy as np
import concourse.bass as bass
import concourse.mybir as mybir
from concourse.bass_utils import run_bass_kernel_spmd

F32 = mybir.dt.float32
BF16 = mybir.dt.bfloat16
AF = mybir.ActivationFunctionType
ALU = mybir.AluOpType

PE, ACT, DVE, POOL, SP = "pe", "act", "dve", "pool", "sp"
GRAN = 256

D = 1024
SEQ = 4096
NT = 1024
NTILES = SEQ // NT
SUB = 512
NSUB = NT // SUB
NBLK = NT // 128
DFF = 2816
NFC = DFF // 128
NG = NFC // 2
DEPTH = 2
MEM = 256
EPS = 1e-6
MASKV = -240000.0

G_FFN1, G_MIX, G_MEM, G_GRP, G_FFN2, G_FINAL = 0, 2, 4, 6, 8, 10
MODE = {"mixer": "interleave", "c_split": (4, 2), "bank": "sim", "kouter": True, "kouter_wo": 4, "sq_dve": 4, "load_act": 2, "store_act": 2, "store_inplace": True, "half_den": True, "b_hold": 3, "early_fc": 12, "late_fin": False, "dconv_at": 6, "load_pipe": True, "pa_dve": (1, 3, 5), "early_a0": 5, "etab": False, "v_nodup": True, "sink_bias": True}


class _Op:
    __slots__ = ("eng", "fn", "deps", "is_dma", "dsem", "dval", "ord", "signal", "id", "t1", "t0", "crit", "tag")

    def __init__(self, eng, fn, is_dma):
        self.eng = eng
        self.fn = fn
        self.deps = {}
        self.is_dma = is_dma
        self.dsem = None
        self.dval = 0
        self.ord = 0
        self.signal = False
        self.id = -1
        self.t1 = 0.0


def _tracked(ap):
    return type(ap.tensor).__name__ in ("SBTensorHandle", "PSumTensorHandle")


def _region(ap):
    pat = ap.ap
    pstride = pat[0][0]
    off = ap.offset
    foff = off % pstride if pstride > 0 else off
    span = 1
    for st, cnt in pat[1:]:
        span += abs(st) * (cnt - 1)
    ds = mybir.dt.size(ap.dtype)
    b0, b1 = foff * ds, (foff + span) * ds
    if type(ap.tensor).__name__ == "PSumTensorHandle":
        b0 = (b0 // 2048) * 2048
        b1 = ((b1 + 2047) // 2048) * 2048
    return ap.name, b0, b1


def _nfree(ap):
    n = 1
    for _, cnt in ap.ap[1:]:
        n *= cnt
    return n


def _nbytes(ap):
    n = 1
    for _, cnt in ap.ap:
        n *= cnt
    return n * mybir.dt.size(ap.dtype)


class Prog:
    NDMA_SEMS = 14
    LAT = 0.15
    DMA_BPU = 300e3

    def __init__(self, nc):
        self.nc = nc
        self.ops = []
        self.last_w = {}
        self.readers = {}
        self.clock = {}
        self.unread = set()
        self.tag = ""
        self.bank_t = [0.0] * 8
        self.dma_free = 0.0

    def add(self, eng, fn, reads=(), writes=(), is_dma=False, cost=0.0):
        op = _Op(eng, fn, is_dma)
        op.id = len(self.ops)
        deps = op.deps
        if eng != PE:
            self._psum_check(reads)
        for ap in reads:
            if ap is None or not _tracked(ap):
                continue
            name, b0, b1 = _region(ap)
            for g in range(b0 // GRAN, (b1 - 1) // GRAN + 1):
                k = (name, g)
                w = self.last_w.get(k)
                if w is not None:
                    deps[w.id] = w
                self.readers.setdefault(k, []).append(op)
        for ap in writes:
            if ap is None or not _tracked(ap):
                continue
            name, b0, b1 = _region(ap)
            for g in range(b0 // GRAN, (b1 - 1) // GRAN + 1):
                k = (name, g)
                w = self.last_w.get(k)
                if w is not None and (w.eng != eng or w.is_dma or is_dma or eng != PE):
                    deps[w.id] = w
                for r in self.readers.get(k, ()):
                    if r is not op and (r.eng != eng or r.is_dma or is_dma or eng != PE):
                        deps[r.id] = r
                self.readers[k] = []
                self.last_w[k] = op
        deps.pop(op.id, None)
        t0 = self.clock.get(eng, 0.0)
        op.crit = None
        op.tag = self.tag
        for d in deps.values():
            if d.t1 + self.LAT > t0:
                t0 = d.t1 + self.LAT
                op.crit = d
        op.t0 = t0
        if is_dma:
            self.clock[eng] = t0 + (1.2 if eng == POOL else 0.06)
            ts = max(t0, self.dma_free)
            nb = max(_nbytes(ap) for ap in list(reads) + list(writes) if ap is not None)
            self.dma_free = ts + nb / self.DMA_BPU
            op.t1 = self.dma_free + 2.0
        else:
            op.t1 = t0 + cost
            self.clock[eng] = op.t1
        for ap in list(reads) + list(writes):
            if ap is not None and type(ap.tensor).__name__ == "PSumTensorHandle":
                _, b0, b1 = _region(ap)
                for b in range(b0 // 2048, (b1 - 1) // 2048 + 1):
                    self.bank_t[b] = op.t1 if self.bank_t[b] == float("inf") else max(self.bank_t[b], op.t1)
        self.ops.append(op)
        return op

    def emit(self, es):
        nc = self.nc
        engs = {PE: nc.tensor, ACT: nc.scalar, DVE: nc.vector, POOL: nc.gpsimd, SP: nc.sync}
        esem = {e: es.enter_context(nc.semaphore("sem_" + e)) for e in engs}
        nd = self.NDMA_SEMS
        dsems = [es.enter_context(nc.semaphore(f"sem_dma{i}")) for i in range(nd)]
        dcount = [0] * nd
        dlast = [None] * nd
        pools = {POOL: list(range(0, nd - 4)), SP: list(range(nd - 4, nd)), ACT: list(range(nd - 4, nd))}
        rrs = {POOL: 0, SP: 0, ACT: 0}

        seq = {}
        scount = {e: 0 for e in engs}
        for op in self.ops:
            if not op.is_dma:
                scount[op.eng] += 1
                seq[op.id] = scount[op.eng]
        wseq = {e: {} for e in engs}
        waited_dma = {e: set() for e in engs}
        ecount = {e: 0 for e in engs}
        plan = []
        for op in self.ops:
            waits = []
            if op.is_dma:
                pl = pools[op.eng]
                j = pl[rrs[op.eng] % len(pl)]
                rrs[op.eng] += 1
                prev = dlast[j]
                if prev is not None and prev.id not in waited_dma[op.eng]:
                    waits.append(("d", prev))
                    waited_dma[op.eng].add(prev.id)
                dcount[j] += 16
                op.dsem = j
                op.dval = dcount[j]
                dlast[j] = op
            best = {}
            for d in op.deps.values():
                if d.is_dma:
                    if d.id in waited_dma[op.eng]:
                        continue
                    waited_dma[op.eng].add(d.id)
                    waits.append(("d", d))
                else:
                    if d.eng == op.eng and op.eng == PE:
                        continue
                    b = best.get(d.eng)
                    if b is None or seq[d.id] > seq[b.id]:
                        best[d.eng] = d
            for pe_, d in best.items():
                if seq[d.id] > wseq[op.eng].get(pe_, 0):
                    wseq[op.eng][pe_] = seq[d.id]
                    waits.append(("e", d))
            plan.append(waits)
        need = set()
        for waits in plan:
            for kind, d in waits:
                if kind == "e":
                    need.add(d.id)
        for op in self.ops:
            if not op.is_dma and op.id in need:
                ecount[op.eng] += 1
                op.ord = ecount[op.eng]
                op.signal = True
        nwaits = 0
        for op, waits in zip(self.ops, plan):
            e = engs[op.eng]
            for kind, d in waits:
                if kind == "d":
                    e.wait_ge(dsems[d.dsem], d.dval)
                else:
                    e.wait_ge(esem[d.eng], d.ord)
                nwaits += 1
            ins = op.fn(e)
            if op.is_dma:
                ins.then_inc(dsems[op.dsem], 16)
            elif op.signal:
                ins.then_inc(esem[op.eng], 1)
        for j in range(nd):
            if dcount[j]:
                nc.sync.wait_ge(dsems[j], dcount[j])
        for en in engs:
            if ecount[en]:
                nc.sync.wait_ge(esem[en], ecount[en])
        self.stats = dict(est_us=max(o.t1 for o in self.ops), nops=len(self.ops), nwaits=nwaits, signals=dict(ecount), per_eng=dict(scount))

    @staticmethod
    def _c_act(ap):
        return 0.10 + 0.00098 * _nfree(ap)

    @staticmethod
    def _c_dve(out, *ins):
        n = _nfree(out)
        if all(mybir.dt.size(a.dtype) == 2 for a in (out,) + ins):
            return 0.08 + 0.0006 * n
        return 0.08 + 0.0012 * n

    def _raw_granules(self, ap):
        pat = ap.ap
        pstride = pat[0][0]
        foff = ap.offset % pstride if pstride > 0 else ap.offset
        span = 1
        for st, cnt in pat[1:]:
            span += abs(st) * (cnt - 1)
        ds = mybir.dt.size(ap.dtype)
        return range(foff * ds // GRAN, ((foff + span) * ds - 1) // GRAN + 1)

    def _psum_check(self, reads, fresh_write=None):
        for ap in reads:
            if ap is not None and type(ap.tensor).__name__ == "PSumTensorHandle":
                for g in self._raw_granules(ap):
                    self.unread.discard(g)
        if fresh_write is not None:
            for g in self._raw_granules(fresh_write):
                assert g not in self.unread, ("PSUM tile overwritten before it was read", g * GRAN // 2048)
                self.unread.add(g)

    def mm(self, out, lhsT, rhs, start=True, stop=True):
        self._psum_check([], out if start else None)
        n = _nfree(rhs)
        cost = max(n / 2400.0, 0.055) + 0.003
        return self.add(PE, lambda e: e.matmul(out, lhsT=lhsT, rhs=rhs, start=start, stop=stop),
                        reads=[lhsT, rhs], writes=[out], cost=cost)

    def transpose(self, out, in_, ident):
        self._psum_check([], out)
        return self.add(PE, lambda e: e.transpose(out, in_, ident), reads=[in_, ident], writes=[out], cost=0.12)

    def act(self, out, in_, func, bias=None, scale=None):
        kw = {}
        if bias is not None:
            kw["bias"] = bias
        if scale is not None:
            kw["scale"] = scale
        rd = [in_] + ([bias] if (bias is not None and not isinstance(bias, (int, float))) else [])
        return self.add(ACT, lambda e: e.activation(out=out, in_=in_, func=func, **kw), reads=rd, writes=[out],
                        cost=self._c_act(out))

    def copy(self, out, in_, eng=DVE):
        if eng == ACT:
            return self.add(ACT, lambda e: e.copy(out=out, in_=in_), reads=[in_], writes=[out], cost=self._c_act(out))
        return self.add(eng, lambda e: e.tensor_copy(out=out, in_=in_), reads=[in_], writes=[out],
                        cost=self._c_dve(out, in_))

    def tt(self, out, in0, in1, op, eng=DVE):
        return self.add(eng, lambda e: e.tensor_tensor(out=out, in0=in0, in1=in1, op=op),
                        reads=[in0, in1], writes=[out], cost=self._c_dve(out, in0, in1))

    def ts1(self, out, in0, s1, op0, eng=DVE):
        rd = [in0] + ([] if isinstance(s1, (int, float)) else [s1])
        return self.add(eng, lambda e: e.tensor_scalar(out=out, in0=in0, scalar1=s1, scalar2=None, op0=op0),
                        reads=rd, writes=[out], cost=self._c_dve(out, in0))

    def stt(self, out, in0, scalar, in1, op0, op1, eng=DVE):
        rd = [in0, in1] + ([] if isinstance(scalar, (int, float)) else [scalar])
        return self.add(eng, lambda e: e.scalar_tensor_tensor(out=out, in0=in0, scalar=scalar, in1=in1,
                                                              op0=op0, op1=op1),
                        reads=rd, writes=[out], cost=self._c_dve(out, in0, in1))

    def recip(self, out, in_):
        return self.add(DVE, lambda e: e.reciprocal(out=out, in_=in_), reads=[in_], writes=[out], cost=3.0)

    def memset(self, ap, val, eng=DVE):
        return self.add(eng, lambda e: e.memset(ap, val), reads=[], writes=[ap], cost=0.08 + 0.0006 * _nfree(ap))

    def dma(self, out, in_, eng=SP):
        return self.add(eng, lambda e: e.dma_start(out=out, in_=in_), reads=[in_], writes=[out], is_dma=True)


class WStream:
    def __init__(self, slots, loads):
        self.slots = slots
        self.loads = loads
        self.R = len(slots)
        self.issued = 0
        self.head = 0
        self.freed = 0
        self.other = None

    def _pump(self):
        lim = min(self.freed + self.R, len(self.loads))
        while self.issued < lim:
            i = self.issued
            self.loads[i][1](self.slots[i % self.R])
            self.issued += 1

    def acquire(self, key):
        i = self.head
        assert self.loads[i][0] == key, (self.loads[i][0], key)
        self._pump()
        assert self.issued > i
        self.head += 1
        return self.slots[i % self.R]

    def release(self, n=1):
        self.freed += n
        self._pump()
        if self.other is not None:
            self.other._pump()


def build_program(ntiles=NTILES, depth=DEPTH, raw_out=False, stop_after=None, dbg=()):
    nc = bass.Bass("TRN2", target_bir_lowering=False)
    P = Prog(nc)

    def din(name, shape):
        return nc.dram_tensor(name, list(shape), F32, kind="ExternalInput").ap()

    x_d = din("x", [SEQ, D])
    mem_d = din("mem", [MEM, D])
    wup_d = {1: din("w_ffn1_up", [DEPTH, D, 2 * DFF]), 2: din("w_ffn2_up", [DEPTH, D, 2 * DFF])}
    wdn_d = {1: din("w_ffn1_down", [DEPTH, DFF, D]), 2: din("w_ffn2_down", [DEPTH, DFF, D])}
    win_d = din("w_in", [DEPTH, D, 1792])
    wmem_d = din("w_mem_kv", [DEPTH, D, 512])
    wout_d = din("w_out", [DEPTH, D, D])
    gains_d = din("gains", [128, 88])
    convw_d = din("convw", [128, 12])
    sinks_d = din("sinks", [1, 16])
    ident_d = din("ident", [128, 128])
    biasc_d = din("bias_cur", [128, 2, 512])
    biasp_d = din("bias_prev", [128, 2, 512])
    out_d = nc.dram_tensor("out", [SEQ, D], F32, kind="ExternalOutput").ap()

    with contextlib.ExitStack() as es:
        def sb(name, shape, dt=F32):
            return es.enter_context(nc.sbuf_tensor(name, list(shape), dt))

        ps = es.enter_context(nc.psum_tensor("ps", [128, 8, 512], F32))
        bank_ctr = [0]
        reserved = set()

        pool_ctr = {}
        ATT = None
        LIN = None
        lin_pool = [None]

        def bank(pool=None):
            if pool is None and MODE["bank"] == "sim":
                now = P.clock.get(PE, 0.0)
                best = None
                for k in range(8):
                    b = (bank_ctr[0] + k) % 8
                    if b in reserved:
                        continue
                    key = max(P.bank_t[b], now)
                    if best is None or key < best[0] - 1e-9:
                        best = (key, b, k)
                bank_ctr[0] += best[2] + 1
                P.bank_t[best[1]] = float("inf")
                return best[1]
            while True:
                if pool is None:
                    b = bank_ctr[0] % 8
                    bank_ctr[0] += 1
                else:
                    name, cand = pool
                    i = pool_ctr.get(name, 0)
                    pool_ctr[name] = i + 1
                    b = cand[i % len(cand)]
                if b not in reserved:
                    return b

        def bank_reserve(pool=None):
            b = bank(pool)
            reserved.add(b)
            return b

        hT = sb("hT", [128, 8, NT])
        xn = sb("xn", [128, 8, NT], BF16)
        big = sb("big", [128, 11264])
        bigb = big[:].bitcast(BF16)
        hid = bigb.rearrange("p (c n) -> p c n", c=NFC)
        qT = bigb[:, 0:4096].rearrange("p (c n) -> p c n", c=4)
        kT = bigb[:, 4096:8192].rearrange("p (c n) -> p c n", c=4)
        qmT = bigb[:, 8192:10240].rearrange("p (c n) -> p c n", c=2)
        Vt = bigb[:, 10240:12288].rearrange("p (b n) -> p b n", b=8)
        ybuf = big[:, 6144:10240].rearrange("p (c n) -> p c n", c=8)
        onorm = big[:, 0:4096].rearrange("p (c n) -> p c n", c=8)
        ostage = big[:, 4096:6144].rearrange("p (b n) -> p b n", b=2)
        NXS = 8
        xbig = big[:, 6144:11264].rearrange("p (b n) -> p b n", b=5)
        xnf = xn[:].bitcast(F32).rearrange("p c n -> p (c n)").rearrange("p (b n) -> p b n", b=4)
        xstage_l = [xnf[:, i, :] for i in range(4)] + [xbig[:, i, :] for i in range(4)]
        mstage_l = [xbig[:, 4, :], ostage[:, 0, :]]
        memT = big[:, 0:2048].rearrange("p (c n) -> p c n", c=8)
        memn = bigb[:, 4096:6144].rearrange("p (c n) -> p c n", c=8)

        NWU, NWD = 4, 5
        wu_slots = [sb(f"wu{i}", [128, 8, 512], BF16) for i in range(NWU)]
        wd_slots = [sb(f"wd{i}", [128, NFC, 128], BF16) for i in range(NWD)]

        kcar = [sb(f"kcar{l}", [128, 4, 128], BF16) for l in range(DEPTH)]
        vcar = [sb(f"vcar{l}", [128, 256], BF16) for l in range(DEPTH)]
        ccar = [[sb(f"ccar{l}_{j}", [128, 2]) for j in range(2)] for l in range(DEPTH)]
        MKz = [sb(f"mkz{l}", [128, 4, MEM], BF16) for l in range(DEPTH)]
        MV = [sb(f"mv{l}", [128, 2, 256], BF16) for l in range(DEPTH)]
        biasC = sb("biasC", [128, 2, 512], BF16)
        biasP = sb("biasP", [128, 2, 512], BF16)
        SR = sb("SR", [128, 4, 512], BF16)
        identf = sb("identf", [128, 128])
        identb = sb("identb", [128, 128], BF16)
        ones_m = {n: sb(f"ones{n}", [128, 128], BF16) for n in (1024, 512, 256, 1)}
        sel = sb("sel", [128, 128], BF16)
        ones_h = [sb(f"ones_h{i}", [128, 128], BF16) for i in range(2)]
        sel_h = [sb(f"sel_h{i}", [128, 128], BF16) for i in range(2)]
        RS = sb("RS", [128, 2, 8], BF16)
        SKB = sb("SKB", [128, 8])
        Gt = sb("Gt", [128, 88])
        CW = sb("CW", [128, 12])
        sk = sb("sk", [64, 16]); ske = sb("ske", [64, 16]); skh = sb("skh", [64, 16], BF16)
        skhf = sb("skhf", [64, 16]); skl = sb("skl", [64, 16])
        Pt = [[sb(f"P{i}_{j}", [128, 512], BF16) for j in range(2)] for i in range(2)]
        rec = [sb(f"rec{i}", [128, 256 if MODE["half_den"] else 512]) for i in range(2)]
        sg = [sb(f"sg{i}", [128, 512]) for i in range(2)]
        NSQ = 8
        sq = [sb(f"sq{i}", [128, 512], BF16) for i in range(NSQ)]
        sq_busy = [False] * NSQ
        sd = [sb(f"sd{i}", [128, 512]) for i in range(2)]
        lnd = sd
        rstd = [sb(f"rstd{i}", [128, 512]) for i in range(2)]
        vbuf = [sb(f"vbuf{j}", [128, 514]) for j in range(2)]
        csb = sb("csb", [128, 512])
        cacc = sb("cacc", [128, 512])

        ctr = {"sq": 0, "nrm": 0, "sg": 0, "ev": 0}

        def sq_get(n):
            for k in range(NSQ):
                i = (ctr["sq"] + k) % NSQ
                if not sq_busy[i]:
                    break
            else:
                raise AssertionError("no free square buffer")
            ctr["sq"] = i + 1
            sq_busy[i] = True
            return i, sq[i][:, 0:n]

        def evac(out, in_, act_of4=2):
            ctr["ev"] += 1
            pat = {0: (), 1: (1,), 2: (1, 3), 3: (0, 1, 3), 4: (0, 1, 2, 3)}[act_of4]
            P.copy(out, in_, eng=ACT if ctr["ev"] % 4 in pat else DVE)

        def v_k(w, l):
            return w[l].rearrange("(kc p) f -> p kc f", p=128)

        def ld_up(which, l, g):
            def f(slot):
                v = v_k(wup_d[which], l)
                P.dma(slot[:, :, 0:256], v[:, :, g * 256:(g + 1) * 256], eng=POOL)
                P.dma(slot[:, :, 256:512], v[:, :, DFF + g * 256:DFF + (g + 1) * 256], eng=POOL)
            return f

        def ld_in(l, i):
            def f(slot):
                v = v_k(win_d, l)
                if i == 0:
                    P.dma(slot[:, :, :], v[:, :, 0:512], eng=POOL)
                elif i == 1:
                    P.dma(slot[:, :, :], v[:, :, 512:1024], eng=POOL)
                else:
                    if i == 2:
                        P.dma(slot[:, :, 0:256], v[:, :, 1024:1280], eng=POOL)
                        base = 1280
                    else:
                        P.dma(slot[:, :, 0:256], v[:, :, 1536:1792], eng=POOL)
                        base = 1408
                    if i == 3 and MODE["v_nodup"]:
                        P.dma(slot[:, :, 256:384], v[:, :, base:base + 128], eng=POOL)
                    else:
                        for kv in range(2):
                            for h in range(2):
                                c0 = 256 + kv * 128 + h * 64
                                P.dma(slot[:, :, c0:c0 + 64], v[:, :, base + kv * 64:base + kv * 64 + 64], eng=POOL)
            return f

        def ld_out(l, i):
            def f(slot):
                P.dma(slot[:, :, :], v_k(wout_d, l)[:, :, i * 512:(i + 1) * 512], eng=POOL)
            return f

        def ld_mem(l):
            def f(slot):
                P.dma(slot[:, :, :], v_k(wmem_d, l)[:, :, :], eng=POOL)
            return f

        def ld_down(which, l, dc):
            def f(slot):
                P.dma(slot[:, :, :], v_k(wdn_d[which], l)[:, :, dc * 128:(dc + 1) * 128], eng=POOL)
            return f

        wu_loads = [(("mem", l), ld_mem(l)) for l in range(depth)]
        wd_loads = []
        for t in range(ntiles):
            for l in range(depth):
                wu_loads += [(("up", 1, l, g), ld_up(1, l, g)) for g in range(NG)]
                wd_loads += [(("down", 1, l, dc), ld_down(1, l, dc)) for dc in range(8)]
                wu_loads += [(("in", l, i), ld_in(l, i)) for i in range(4)]
                wu_loads += [(("out", l, i), ld_out(l, i)) for i in range(2)]
                wu_loads += [(("up", 2, l, g), ld_up(2, l, g)) for g in range(NG)]
                wd_loads += [(("down", 2, l, dc), ld_down(2, l, dc)) for dc in range(8)]
        WU = WStream(wu_slots, wu_loads)
        WD = WStream(wd_slots, wd_loads)

        def norm_split(srcs, gidx, gc0, nfeat, dsts, n):
            sqs = []
            for s_ in srcs:
                i, q = sq_get(n)
                P.act(q, s_, AF.Square)
                sqs.append((i, q))

            def part_b():
                i0 = ctr["nrm"] % 2
                ctr["nrm"] += 1
                b = bank(lin_pool[0])
                pss = ps[:, b, 0:n]
                last = len(sqs) - 1
                for k, (i, q) in enumerate(sqs):
                    P.mm(pss, ones_m[nfeat][:], q, start=(k == 0), stop=(k == last))
                    sq_busy[i] = False
                P.act(sd[i0][:, 0:n], pss, AF.Ln, bias=EPS, scale=1.0)
                P.act(rstd[i0][:, 0:n], sd[i0][:, 0:n], AF.Exp, scale=-0.5)
                for k, (s_, d_) in enumerate(zip(srcs, dsts)):
                    gcol = Gt[:, gidx * 8 + gc0 + k:gidx * 8 + gc0 + k + 1]
                    P.stt(d_, s_, gcol, rstd[i0][:, 0:n], ALU.mult, ALU.mult)
            return part_b

        def norm(srcs, gidx, gc0, nfeat, dsts, n):
            norm_split(srcs, gidx, gc0, nfeat, dsts, n)()

        class NormAcc:
            def __init__(self, s):
                self.s = s
                self.b = bank_reserve(lin_pool[0])
                self.cnt = 0
                self.half = None
                self.pending = []
                self.i0 = None

            def add(self, src, on_dve=False):
                i, q = sq_get(SUB)
                if on_dve:
                    P.tt(q, src, src, ALU.mult)
                else:
                    P.act(q, src, AF.Square)
                if self.half is None:
                    self.half = (i, q)
                else:
                    ia, qa = self.half
                    P.tt(qa, qa, q, ALU.add)
                    sq_busy[i] = False
                    self.half = None
                    self.pending.append((ia, qa))

            def flush(self):
                for i, q in self.pending:
                    P.mm(ps[:, self.b, :], ones_m[1024][:], q, start=(self.cnt == 0), stop=(self.cnt == 3))
                    sq_busy[i] = False
                    self.cnt += 1
                self.pending = []

            def finish_a(self):
                if self.i0 is not None:
                    return
                self.flush()
                assert self.cnt == 4 and self.half is None
                i0 = ctr["nrm"] % 2
                ctr["nrm"] += 1
                self.i0 = i0
                P.act(sd[i0][:], ps[:, self.b, :], AF.Ln, bias=EPS, scale=1.0)
                P.act(rstd[i0][:], sd[i0][:], AF.Exp, scale=-0.5)
                reserved.discard(self.b)

            def finish(self, gidx, dsts):
                self.finish_a()
                i0 = self.i0
                ssl = slice(self.s * SUB, (self.s + 1) * SUB)
                for c in range(8):
                    gcol = Gt[:, gidx * 8 + c:gidx * 8 + c + 1]
                    P.stt(dsts[c], hT[:, c, ssl], gcol, rstd[i0][:], ALU.mult, ALU.mult,
                          eng=(POOL if ("poolnorm" in dbg and c % 2 == 1) else DVE))

        def plain_accs():
            accs = [NormAcc(s) for s in range(NSUB)]
            for s in range(NSUB):
                for c in range(8):
                    accs[s].add(hT[:, c, s * SUB:(s + 1) * SUB])
                    accs[s].flush()
            return accs

        def ffn(which, l, accs, nxt=None):
            gidx = (G_FFN1 if which == 1 else G_FFN2) + l
            P.tag = "ffn-up"
            late = MODE["late_fin"] and MODE["kouter"]
            for s in range(NSUB):
                ssl = slice(s * SUB, (s + 1) * SUB)
                if s == 0 or not late:
                    accs[s].finish(gidx, [xn[:, c, ssl] for c in range(8)])
            for g in range(NG):
                w = WU.acquire(("up", which, l, g))
                for s in range(NSUB):
                    ssl = slice(s * SUB, (s + 1) * SUB)
                    if g == 0 and s == 1 and late:
                        accs[1].finish(gidx, [xn[:, c, ssl] for c in range(8)])
                    if g == 0 and MODE["kouter"]:
                        bks = [(bank(), bank()) for j in range(2)]
                        for k in range(8):
                            for j in range(2):
                                P.mm(ps[:, bks[j][0], :], w[:, k, j * 128:(j + 1) * 128], xn[:, k, ssl],
                                     start=(k == 0), stop=(k == 7))
                                P.mm(ps[:, bks[j][1], :], w[:, k, 256 + j * 128:256 + (j + 1) * 128], xn[:, k, ssl],
                                     start=(k == 0), stop=(k == 7))
                        for j in range(2):
                            sgb = sg[ctr["sg"] % 2]
                            ctr["sg"] += 1
                            P.act(sgb[:], ps[:, bks[j][0], :], AF.Silu)
                            P.tt(hid[:, 2 * g + j, ssl], sgb[:], ps[:, bks[j][1], :], ALU.mult)
                        continue
                    for j in range(2):
                        fc = 2 * g + j
                        bg, bu = bank(), bank()
                        for k in range(8):
                            P.mm(ps[:, bg, :], w[:, k, j * 128:(j + 1) * 128], xn[:, k, ssl], start=(k == 0), stop=(k == 7))
                        for k in range(8):
                            P.mm(ps[:, bu, :], w[:, k, 256 + j * 128:256 + (j + 1) * 128], xn[:, k, ssl],
                                 start=(k == 0), stop=(k == 7))
                        sgb = sg[ctr["sg"] % 2]
                        ctr["sg"] += 1
                        P.act(sgb[:], ps[:, bg, :], AF.Silu)
                        P.tt(hid[:, fc, ssl], sgb[:], ps[:, bu, :], ALU.mult)
                WU.release()
            nacc = [NormAcc(s) for s in range(NSUB)]
            pend = []
            P.tag = "ffn-down"
            for dc in range(8):
                w = WD.acquire(("down", which, l, dc))
                for s in range(NSUB):
                    ssl = slice(s * SUB, (s + 1) * SUB)
                    b = bank()
                    for fc in range(NFC):
                        P.mm(ps[:, b, :], w[:, fc, :], hid[:, fc, ssl], start=(fc == 0), stop=(fc == NFC - 1))
                        if (nxt is not None and dc == 7 and s == NSUB - 1 and MODE["early_fc"]
                                and fc == MODE["early_fc"] - 1):
                            nacc[0].finish_a()
                    if len(pend) >= 2:
                        nacc[pend.pop(0)].flush()
                    P.stt(hT[:, dc, ssl], ps[:, b, :], 0.5, hT[:, dc, ssl], ALU.mult, ALU.add)
                    nacc[s].add(hT[:, dc, ssl])
                    pend.append(s)
                WD.release()
            return nacc

        x_issued = {}

        def x_dma(t, upto):
            i = x_issued.get(t, 0)
            while i < min(upto, NBLK):
                P.dma(xstage_l[i % NXS], x_d[t * NT + i * 128:t * NT + (i + 1) * 128, :], eng=SP)
                i += 1
            x_issued[t] = i

        def load_x_accs(t):
            P.tag = "load"
            accs = []
            bps = NBLK // NSUB
            for b in range(NBLK):
                x_dma(t, b + NXS)
                st = xstage_l[b % NXS]
                for half in range(2):
                    bk = bank()
                    for cc in range(4):
                        c = half * 4 + cc
                        P.transpose(ps[:, bk, cc * 128:(cc + 1) * 128], st[:, c * 128:(c + 1) * 128], identf[:])
                    src = ps[:, bk, :].rearrange("p (c n) -> p c n", c=4)
                    evac(hT[:, half * 4:half * 4 + 4, b * 128:(b + 1) * 128], src, MODE["load_act"])
                if b % bps == bps - 1:
                    s = b // bps
                    acc = NormAcc(s)
                    for c in range(8):
                        acc.add(hT[:, c, s * SUB:(s + 1) * SUB], on_dve=(c in MODE["pa_dve"]))
                        acc.flush()
                    accs.append(acc)
            return accs

        def load_x(t):
            P.tag = "load"
            for b in range(NBLK):
                x_dma(t, b + NXS)
                st = xstage_l[b % NXS]
                for half in range(2):
                    bk = bank()
                    for cc in range(4):
                        c = half * 4 + cc
                        P.transpose(ps[:, bk, cc * 128:(cc + 1) * 128], st[:, c * 128:(c + 1) * 128], identf[:])
                    src = ps[:, bk, :].rearrange("p (c n) -> p c n", c=4)
                    evac(hT[:, half * 4:half * 4 + 4, b * 128:(b + 1) * 128], src, MODE["load_act"])

        ostage6 = big[:, 0:6144].rearrange("p (b n) -> p b n", b=6)

        def store_out_inplace(t, accs):
            P.tag = "store"
            for s in range(NSUB):
                ssl = slice(s * SUB, (s + 1) * SUB)
                accs[s].finish(G_FINAL, [hT[:, c, ssl] for c in range(8)])
            for s in range(NSUB):
                for bb in range(4):
                    i = s * 4 + bb
                    ost = ostage6[:, i % 6, :]
                    c0 = s * SUB + bb * 128
                    for half in range(2):
                        bk = bank()
                        for cc in range(4):
                            c = half * 4 + cc
                            P.transpose(ps[:, bk, cc * 128:(cc + 1) * 128], hT[:, c, c0:c0 + 128], identf[:])
                        evac(ost[:, half * 512:(half + 1) * 512], ps[:, bk, :], MODE["store_act"])
                    r0 = t * NT + c0
                    P.dma(out_d[r0:r0 + 128, :], ost, eng=SP)

        def store_out(t, accs):
            if MODE["store_inplace"] and not raw_out:
                return store_out_inplace(t, accs)
            P.tag = "store"
            for s in range(NSUB):
                ssl = slice(s * SUB, (s + 1) * SUB)
                if raw_out:
                    accs[s].flush()
                    reserved.discard(accs[s].b)
                    srcs = [hT[:, c, ssl] for c in range(8)]
                else:
                    accs[s].finish(G_FINAL, [onorm[:, c, :] for c in range(8)])
                    srcs = [onorm[:, c, :] for c in range(8)]
                for bb in range(4):
                    ost = ostage[:, bb % 2, :]
                    for half in range(2):
                        bk = bank()
                        for cc in range(4):
                            c = half * 4 + cc
                            P.transpose(ps[:, bk, cc * 128:(cc + 1) * 128], srcs[c][:, bb * 128:(bb + 1) * 128], identf[:])
                        evac(ost[:, half * 512:(half + 1) * 512], ps[:, bk, :], MODE["store_act"])
                    r0 = t * NT + s * SUB + bb * 128
                    P.dma(out_d[r0:r0 + 128, :], ost, eng=SP)

        def mixer(l, t, accs):
            P.tag = "mix-A"
            late = MODE["late_fin"] and MODE["kouter"] and MODE["mixer"] != "seq"
            for s in range(NSUB):
                ssl = slice(s * SUB, (s + 1) * SUB)
                if s == 0 or not late:
                    accs[s].finish(G_MIX + l, [xn[:, c, ssl] for c in range(8)])
            for kv in range(2):
                P.memset(kT[64:128, kv * 2 + 0, :], 0.0)
                P.memset(kT[0:64, kv * 2 + 1, :], 0.0)
            dconv = [None] * NSUB
            dgrp = [None] * NSUB
            w0 = WU.acquire(("in", l, 0))
            w1 = WU.acquire(("in", l, 1))
            w2 = WU.acquire(("in", l, 2))
            w3 = WU.acquire(("in", l, 3))

            def proj(out_bank_ap, wslot, c0, rhs_of_k, n=128):
                for k in range(8):
                    P.mm(out_bank_ap, wslot[:, k, c0:c0 + n], rhs_of_k(k), start=(k == 0), stop=(k == 7))

            def proj_gen(s):
                ssl = slice(s * SUB, (s + 1) * SUB)
                xs = lambda k: xn[:, k, ssl]
                last = (s == NSUB - 1)
                for j in range(2):
                    pC, pX, pB = bank(lin_pool[0]), bank(lin_pool[0]), bank(lin_pool[0])
                    reserved.update((pC, pX, pB))
                    if s == 0 and j == 0 and MODE["kouter"]:
                        for k in range(8):
                            P.mm(ps[:, pC, :], w0[:, k, 256:384], xs(k), start=(k == 0), stop=(k == 7))
                            P.mm(ps[:, pX, :], w1[:, k, 0:128], xs(k), start=(k == 0), stop=(k == 7))
                            P.mm(ps[:, pB, :], w0[:, k, 0:128], xs(k), start=(k == 0), stop=(k == 7))
                        yield
                        yield
                    else:
                        proj(ps[:, pC, :], w0, 256 + j * 128, xs)
                        yield
                        proj(ps[:, pX, :], w1, j * 128, xs)
                        yield
                        proj(ps[:, pB, :], w0, j * 128, xs)
                    if j == 0 and s > 0 and dconv[s - 1] is not None:
                        dconv[s - 1]()
                        dconv[s - 1] = None
                    vb = vbuf[j]
                    cw = CW[:, (l * 2 + j) * 3:(l * 2 + j) * 3 + 3]
                    P.copy(csb[:], ps[:, pC, :], eng=ACT)
                    P.copy(vb[:, 0:2], ccar[l][j][:], eng=DVE)
                    P.tt(vb[:, 2:514], csb[:], ps[:, pX, :], ALU.mult)
                    P.ts1(cacc[:], vb[:, 2:514], cw[:, 2:3], ALU.mult)
                    P.stt(cacc[:], vb[:, 1:513], cw[:, 1:2], cacc[:], ALU.mult, ALU.add)
                    P.stt(cacc[:], vb[:, 0:512], cw[:, 0:1], cacc[:], ALU.mult, ALU.add)
                    P.tt(ybuf[:, j, :], cacc[:], ps[:, pB, :], ALU.mult)
                    P.copy(ccar[l][j][:], vb[:, 512:514], eng=DVE)
                    for b_ in (pC, pX, pB):
                        reserved.discard(b_)
                    yield
                if last:
                    WU.release(1)
                for c in range(4):
                    b = bank(lin_pool[0])
                    if c < 2:
                        proj(ps[:, b, :], w1, 256 + c * 128, xs)
                    else:
                        proj(ps[:, b, :], w2, (c - 2) * 128, xs)
                    evac(qT[:, c, ssl], ps[:, b, :])
                    if c == 1 and last:
                        WU.release(1)
                    yield
                for kv in range(2):
                    b = bank(lin_pool[0])
                    proj(ps[:, b, :], w2, 256 + kv * 128, xs)
                    P.copy(kT[0:64, kv * 2 + 0, ssl], ps[0:64, b, :], eng=ACT)
                    P.copy(kT[64:128, kv * 2 + 1, ssl], ps[64:128, b, :], eng=DVE)
                    yield
                if last:
                    WU.release(1)
                for c in range(2):
                    b = bank(lin_pool[0])
                    proj(ps[:, b, :], w3, c * 128, xs)
                    evac(qmT[:, c, ssl], ps[:, b, :])
                    yield
                for bl in range(4):
                    blk = s * 4 + bl
                    b = bank(lin_pool[0])
                    if MODE["v_nodup"]:
                        for k in range(8):
                            P.mm(ps[:, b, 0:128], xn[:, k, blk * 128:(blk + 1) * 128], w3[:, k, 256:384],
                                 start=(k == 0), stop=(k == 7))
                        vsrc = ps[:, b, 0:128].rearrange("p (kv d) -> p kv d", kv=2).unsqueeze(2)
                        evac(Vt[:, blk, :].rearrange("p (kv h d) -> p kv h d", kv=2, h=2),
                             vsrc.to_broadcast([128, 2, 2, 64]))
                    else:
                        for k in range(8):
                            P.mm(ps[:, b, 0:256], xn[:, k, blk * 128:(blk + 1) * 128], w3[:, k, 256:512],
                                 start=(k == 0), stop=(k == 7))
                        evac(Vt[:, blk, :], ps[:, b, 0:256])
                    yield
                if last:
                    WU.release(1)
                dconv[s] = norm_split([ybuf[:, 0, :], ybuf[:, 1, :]], G_GRP + l, 0, 256,
                                      [xn[:, 0, ssl], xn[:, 1, ssl]], SUB)

            def kblk(v, blk):
                return kT[:, v, blk * 128:(blk + 1) * 128] if blk >= 0 else kcar[l][:, v, :]

            def vblk(kv, blk):
                return Vt[:, blk, kv * 128:(kv + 1) * 128] if blk >= 0 else vcar[l][:, kv * 128:(kv + 1) * 128]

            def swa1(u):
                blk, kv, pi = u["blk"], u["kv"], u["pi"]
                qsl = slice(blk * 128, (blk + 1) * 128)
                has_prev = (t * NBLK + blk) > 0
                u["has_prev"] = has_prev
                et = MODE["etab"]
                bc = bank(ATT)
                if not et:
                    P.mm(ps[:, bc, :], identb[:], biasC[:, kv, :], start=True, stop=False)
                for par in range(2):
                    P.mm(ps[:, bc, par * 256:(par + 1) * 256].rearrange("p (j q) -> p j q", j=2),
                         kblk(kv * 2 + par, blk), qT[:, kv * 2:kv * 2 + 2, qsl],
                         start=bool(et), stop=(True if et else par == 1))
                P.act(Pt[pi][0][:], ps[:, bc, :], AF.Exp, scale=0.125)
                if et:
                    P.tt(Pt[pi][0][:], Pt[pi][0][:], biasC[:, kv, :], ALU.mult)
                if has_prev:
                    bp = bank(ATT)
                    if not et:
                        P.mm(ps[:, bp, :], identb[:], biasP[:, kv, :], start=True, stop=False)
                    for par in range(2):
                        P.mm(ps[:, bp, par * 256:(par + 1) * 256].rearrange("p (j q) -> p j q", j=2),
                             kblk(kv * 2 + par, blk - 1), qT[:, kv * 2:kv * 2 + 2, qsl],
                             start=bool(et), stop=(True if et else par == 1))
                    P.act(Pt[pi][1][:], ps[:, bp, :], AF.Exp, scale=0.125)
                    if et:
                        P.tt(Pt[pi][1][:], Pt[pi][1][:], biasP[:, kv, :], ALU.mult)

            def finish(u, bo, bd, c0, grouped=False, sink_col=None):
                pi = u["pi"]
                bl = u["blk"] % 4
                half = MODE["half_den"]
                nd = 256 if half else 512
                if sink_col is not None:
                    for j in range(2):
                        P.act(lnd[pi][:, j * 128:(j + 1) * 128], ps[:, bd, j * 128:(j + 1) * 128], AF.Ln,
                              bias=SKB[:, sink_col + j:sink_col + j + 1], scale=1.0)
                else:
                    P.act(lnd[pi][:, 0:nd], ps[:, bd, 0:nd], AF.Ln)
                P.act(rec[pi][:, 0:nd], lnd[pi][:, 0:nd], AF.Exp, scale=-1.0)
                for par in range(2):
                    p0 = par * 64
                    if grouped:
                        src = ps[p0:p0 + 64, bo, par * 256:(par + 1) * 256].rearrange("p (g q) -> p g q", g=2)
                    else:
                        src = ps[p0:p0 + 64, bo, :].rearrange("p (g q) -> p g q", g=4)[:, par::2, :]
                    if half:
                        rv = rec[pi][p0:p0 + 64, 0:256].rearrange("p (g q) -> p g q", g=2)
                    elif grouped:
                        rv = rec[pi][p0:p0 + 64, par * 256:(par + 1) * 256].rearrange("p (g q) -> p g q", g=2)
                    else:
                        rv = rec[pi][p0:p0 + 64, :].rearrange("p (g q) -> p g q", g=4)[:, par::2, :]
                    dst = ybuf[p0:p0 + 64, c0:c0 + 2, bl * 128:(bl + 1) * 128]
                    P.tt(dst, src, rv, ALU.mult)

            def swa2(u):
                blk, kv, pi = u["blk"], u["kv"], u["pi"]
                hp = u["has_prev"]
                bo, bd = bank(ATT), bank(ATT)
                P.mm(ps[:, bo, :], vblk(kv, blk), Pt[pi][0][:], start=True, stop=not hp)
                if hp:
                    P.mm(ps[:, bo, :], vblk(kv, blk - 1), Pt[pi][1][:], start=False, stop=True)
                if MODE["half_den"]:
                    dn = ps[:, bd, 0:256]
                    first = True
                    sb_ = MODE["sink_bias"]
                    pts = [Pt[pi][0]] + ([Pt[pi][1]] if hp else [])
                    for ip, pt in enumerate(pts):
                        for h in range(2):
                            P.mm(dn, ones_h[h][:], pt[:, h * 256:(h + 1) * 256], start=first,
                                 stop=(sb_ and ip == len(pts) - 1 and h == 1))
                            first = False
                    if not sb_:
                        for h in range(2):
                            P.mm(dn, sel_h[h][:], SR[:, l * 2 + kv, h * 256:(h + 1) * 256], start=False, stop=(h == 1))
                else:
                    P.mm(ps[:, bd, :], ones_m[1][:], Pt[pi][0][:], start=True, stop=False)
                    if hp:
                        P.mm(ps[:, bd, :], ones_m[1][:], Pt[pi][1][:], start=False, stop=False)
                    P.mm(ps[:, bd, :], sel[:], SR[:, l * 2 + kv, :], start=False, stop=True)
                finish(u, bo, bd, 2 + kv * 2, grouped=True,
                       sink_col=((l * 2 + kv) * 2 if (MODE["half_den"] and MODE["sink_bias"]) else None))

            def mem1(u):
                blk, pi = u["blk"], u["pi"]
                qsl = slice(blk * 128, (blk + 1) * 128)
                for mc in range(2):
                    b = bank(ATT)
                    for hm in range(4):
                        P.mm(ps[:, b, hm * 128:(hm + 1) * 128], MKz[l][:, hm, mc * 128:(mc + 1) * 128],
                             qmT[:, hm // 2, qsl], start=True, stop=True)
                    P.act(Pt[pi][mc][:], ps[:, b, :], AF.Exp, scale=0.125)

            def mem2(u):
                pi = u["pi"]
                bo, bd = bank(ATT), bank(ATT)
                for c in range(2):
                    for mc in range(2):
                        P.mm(ps[:, bo, c * 256:(c + 1) * 256], MV[l][:, mc, c * 128:(c + 1) * 128],
                             Pt[pi][mc][:, c * 256:(c + 1) * 256], start=(mc == 0), stop=(mc == 1))
                if MODE["half_den"]:
                    dn = ps[:, bd, 0:256].rearrange("p (g q) -> p g q", g=2)
                    for mc in range(2):
                        pv = Pt[pi][mc][:].rearrange("p (g q) -> p g q", g=4)
                        for h in range(2):
                            P.mm(dn, ones_h[h][:], pv[:, h::2, :], start=(mc == 0 and h == 0), stop=(mc == 1 and h == 1))
                else:
                    for mc in range(2):
                        P.mm(ps[:, bd, :], ones_m[1][:], Pt[pi][mc][:], start=(mc == 0), stop=(mc == 1))
                finish(u, bo, bd, 6)

            def attn_gen(s, pre_hook):
                ssl = slice(s * SUB, (s + 1) * SUB)
                units = []
                for bl in range(4):
                    blk = s * 4 + bl
                    units.append(dict(kind="swa", blk=blk, kv=0))
                    units.append(dict(kind="swa", blk=blk, kv=1))
                    units.append(dict(kind="mem", blk=blk))
                prev = None
                for i, u in enumerate(units):
                    u["pi"] = i % 2
                    (swa1 if u["kind"] == "swa" else mem1)(u)
                    pre_hook(i)
                    if prev is not None:
                        (swa2 if prev["kind"] == "swa" else mem2)(prev)
                    prev = u
                    yield
                (swa2 if prev["kind"] == "swa" else mem2)(prev)
                nb1 = norm_split([ybuf[:, 2 + c, :] for c in range(4)], G_GRP + l, 2, 512,
                                 [xn[:, 2 + c, ssl] for c in range(4)], SUB)
                nb2 = norm_split([ybuf[:, 6 + c, :] for c in range(2)], G_GRP + l, 6, 256,
                                 [xn[:, 6 + c, ssl] for c in range(2)], SUB)
                dgrp[s] = (lambda a, b: (lambda: (a(), b())))(nb1, nb2)

            nacc = [None] * NSUB
            pend = []
            wo = []

            def wout_gen(s):
                ssl = slice(s * SUB, (s + 1) * SUB)
                nacc[s] = NormAcc(s)
                kob = []
                if s == 1 and MODE["kouter_wo"]:
                    nko = MODE["kouter_wo"]
                    kob = [bank(lin_pool[0]) for _ in range(nko)]
                    for k in range(8):
                        for dc in range(nko):
                            P.mm(ps[:, kob[dc], :], wo[dc // 4][:, k, (dc % 4) * 128:(dc % 4 + 1) * 128], xn[:, k, ssl],
                                 start=(k == 0), stop=(k == 7))
                for dc in range(8):
                    w = wo[dc // 4]
                    if dc < len(kob):
                        b = kob[dc]
                    else:
                        b = bank(lin_pool[0])
                        for k in range(8):
                            P.mm(ps[:, b, :], w[:, k, (dc % 4) * 128:(dc % 4 + 1) * 128], xn[:, k, ssl],
                                 start=(k == 0), stop=(k == 7))
                    if len(pend) >= 2:
                        nacc[pend.pop(0)].flush()
                    P.tt(hT[:, dc, ssl], ps[:, b, :], hT[:, dc, ssl], ALU.add)
                    nacc[s].add(hT[:, dc, ssl], on_dve=(s == 0 and dc < MODE["sq_dve"]))
                    pend.append(s)
                    if s == 1 and MODE["early_a0"] and dc == MODE["early_a0"] - 1:
                        nacc[0].finish_a()
                    yield

            def step(g):
                try:
                    next(g)
                    return True
                except StopIteration:
                    return False

            def merge(ga, gb, na, nb, b_start, b_hold, after0=None):
                done_b = 0
                slots = na - b_start
                for i in range(na):
                    assert step(ga)
                    if i == 0 and after0 is not None:
                        after0()
                    if i >= b_start:
                        target = ((i - b_start + 1) * (nb - b_hold)) // slots
                        while done_b < target:
                            assert step(gb)
                            done_b += 1
                assert not step(ga)
                return done_b

            def drain(g):
                while step(g):
                    pass

            def hook_none(i):
                pass

            def hook_prev(i):
                if i == 1:
                    dgrp[0]()
                    dgrp[0] = None
                if i == MODE["dconv_at"]:
                    dconv[1]()
                    dconv[1] = None

            assert NSUB == 2
            if MODE["mixer"] == "seq":
                def hook_c1(i):
                    if i == 1:
                        dconv[1]()
                        dconv[1] = None

                def hook_g0(i):
                    if i == 1:
                        dgrp[0]()
                        dgrp[0] = None
                drain(proj_gen(0))
                drain(proj_gen(1))
                drain(attn_gen(0, hook_c1))
                drain(attn_gen(1, hook_g0))
                P.copy(kcar[l][:, :, :], kT[:, :, (NBLK - 1) * 128:NBLK * 128], eng=DVE)
                P.copy(vcar[l][:, :], Vt[:, NBLK - 1, :], eng=DVE)
                wo.append(WU.acquire(("out", l, 0)))
                wo.append(WU.acquire(("out", l, 1)))
                drain(wout_gen(0))
                dgrp[1]()
                dgrp[1] = None
                drain(wout_gen(1))
            else:
                nC, hC = MODE["c_split"]
                g0 = proj_gen(0)
                if late:
                    step(g0)
                    accs[1].finish(G_MIX + l, [xn[:, c, SUB:2 * SUB] for c in range(8)])
                drain(g0)
                lin_pool[0] = LIN
                P.tag = "mix-B"
                gbB = proj_gen(1)
                merge(attn_gen(0, hook_none), gbB, 12, 18, 0, MODE["b_hold"])
                wo.append(WU.acquire(("out", l, 0)))
                wo.append(WU.acquire(("out", l, 1)))
                P.tag = "mix-C"
                gb = wout_gen(0)
                merge(attn_gen(1, hook_prev), gb, 12, 8, 2, nC, after0=lambda: drain(gbB))
                P.tag = "mix-D"
                P.copy(kcar[l][:, :, :], kT[:, :, (NBLK - 1) * 128:NBLK * 128], eng=DVE)
                P.copy(vcar[l][:, :], Vt[:, NBLK - 1, :], eng=DVE)
                for _ in range(hC):
                    step(gb)
                dgrp[1]()
                dgrp[1] = None
                drain(gb)
                lin_pool[0] = None
                drain(wout_gen(1))
            WU.release(2)
            return nacc

        P.dma(identf[:], ident_d, eng=SP)
        x_dma(0, NXS)
        P.dma(Gt[:], gains_d, eng=SP)
        P.dma(CW[:], convw_d, eng=SP)
        P.dma(biasC[:], biasc_d, eng=POOL)
        P.dma(biasP[:], biasp_d, eng=POOL)
        if MODE["etab"]:
            for kv in range(2):
                P.act(biasC[:, kv, :], biasC[:, kv, :], AF.Exp, scale=0.125)
                P.act(biasP[:, kv, :], biasP[:, kv, :], AF.Exp, scale=0.125)
        P.copy(identb[:], identf[:], eng=DVE)
        for n in (1024, 512, 256, 1):
            P.memset(ones_m[n][:], 1.0 / n)
        P.memset(sel[:], 0.0)
        P.memset(sel[0:1, :], 1.0)
        P.memset(sel[32:33, :], 1.0)
        for i in range(2):
            P.memset(ones_h[i][:], 0.0)
            P.memset(ones_h[i][:, i * 64:(i + 1) * 64], 1.0)
            P.memset(sel_h[i][:], 0.0)
            P.memset(sel_h[i][0:1, i * 64:(i + 1) * 64], 1.0)
            P.memset(sel_h[i][32:33, i * 64:(i + 1) * 64], 1.0)
        for l in range(DEPTH):
            P.memset(MKz[l][:], 0.0)
            for j in range(2):
                P.memset(ccar[l][j][:], 0.0)
        P.memset(sk[:], 0.0)
        P.dma(sk[0:1, :], sinks_d, eng=SP)
        P.dma(sk[32:33, :], sinks_d, eng=SP)
        P.act(ske[:], sk[:], AF.Exp)
        P.copy(skh[:], ske[:])
        P.copy(skhf[:], skh[:])
        P.tt(skl[:], ske[:], skhf[:], ALU.subtract)
        P.memset(SR[:], 0.0)
        for lk in range(4):
            for par in range(2):
                src_hi = skh[0:1, lk * 4 + par:lk * 4 + 4:2].unsqueeze(2).to_broadcast([1, 2, 128])
                src_lo = skl[32:33, lk * 4 + par:lk * 4 + 4:2].unsqueeze(2).to_broadcast([1, 2, 128])
                P.copy(SR[0:1, lk, par * 256:(par + 1) * 256].rearrange("p (g q) -> p g q", g=2), src_hi)
                P.copy(SR[32:33, lk, par * 256:(par + 1) * 256].rearrange("p (g q) -> p g q", g=2), src_lo)
        if MODE["sink_bias"]:
            P.memset(RS[:], 0.0)
            for par in range(2):
                P.copy(RS[0:1, par, :], skh[0:1, par:16:2])
                P.copy(RS[32:33, par, :], skl[32:33, par:16:2])
            bsk = bank()
            for par in range(2):
                P.mm(ps[:, bsk, 0:8], sel_h[par][:], RS[:, par, :], start=(par == 0), stop=(par == 1))
            P.copy(SKB[:], ps[:, bsk, 0:8])
        for mc in range(2):
            P.dma(mstage_l[mc], mem_d[mc * 128:(mc + 1) * 128, :], eng=SP)
        for mc in range(2):
            for half in range(2):
                bk = bank()
                for cc in range(4):
                    c = half * 4 + cc
                    P.transpose(ps[:, bk, cc * 128:(cc + 1) * 128], mstage_l[mc][:, c * 128:(c + 1) * 128], identf[:])
                evac(memT[:, half * 4:half * 4 + 4, mc * 128:(mc + 1) * 128],
                     ps[:, bk, :].rearrange("p (c n) -> p c n", c=4))
        for l in range(depth):
            wm = WU.acquire(("mem", l))
            norm([memT[:, c, :] for c in range(8)], G_MEM + l, 0, 1024, [memn[:, c, :] for c in range(8)], MEM)
            for c in range(2):
                b = bank()
                for k in range(8):
                    P.mm(ps[:, b, 0:256], wm[:, k, c * 128:(c + 1) * 128], memn[:, k, :], start=(k == 0), stop=(k == 7))
                P.copy(MKz[l][0:64, 2 * c, :], ps[0:64, b, 0:256], eng=ACT)
                P.copy(MKz[l][64:128, 2 * c + 1, :], ps[64:128, b, 0:256], eng=DVE)
            for mc in range(2):
                b = bank()
                for k in range(8):
                    P.mm(ps[:, b, 0:256], memn[:, k, mc * 128:(mc + 1) * 128], wm[:, k, 256:512],
                         start=(k == 0), stop=(k == 7))
                evac(MV[l][:, mc, :], ps[:, b, 0:256])
            WU.release()

        done = False
        WD._pump()
        for t in range(ntiles):
            if MODE["load_pipe"]:
                accs = load_x_accs(t)
            else:
                load_x(t)
                accs = plain_accs()
            for l in range(depth):
                dst_xn = lambda c, ssl: xn[:, c, ssl]
                dst_ht = lambda c, ssl: hT[:, c, ssl]
                accs = ffn(1, l, accs, nxt=(G_MIX + l, dst_xn))
                if stop_after == ("ffn1", l):
                    done = True
                    break
                accs = mixer(l, t, accs)
                if stop_after == ("mixer", l):
                    done = True
                    break
                if l + 1 < depth:
                    nxt2 = (G_FFN1 + l + 1, dst_xn)
                elif MODE["store_inplace"] and not raw_out:
                    nxt2 = (G_FINAL, dst_ht)
                else:
                    nxt2 = None
                accs = ffn(2, l, accs, nxt=nxt2)
            if t + 1 < ntiles and not done:
                x_dma(t + 1, NXS)
            store_out(t, accs)
            if done:
                break

        P.emit(es)
        build_program.stats = P.stats
        build_program.prog = P
    return nc


def host_layout(inputs):
    f32 = np.float32
    g = lambda k: np.asarray(inputs[k], dtype=f32)
    gl = [g("g_ffn1")[0], g("g_ffn1")[1], g("g_mix")[0], g("g_mix")[1], g("g_mem")[0], g("g_mem")[1],
          g("g_grp")[0], g("g_grp")[1], g("g_ffn2")[0], g("g_ffn2")[1], g("g_final")]
    gains = np.ascontiguousarray(np.stack(gl, 0).reshape(11, 8, 128).transpose(2, 0, 1).reshape(128, 88))
    cw = g("conv_w")
    convw = np.ascontiguousarray(cw.reshape(DEPTH, 3, 2, 128).transpose(3, 0, 2, 1).reshape(128, 12))
    sinks = np.ascontiguousarray(g("sinks").reshape(1, 16))
    ident = np.eye(128, dtype=f32)
    slopes = np.array([2.0 ** (-8.0 * (i + 1) / 8) for i in range(8)], dtype=np.float64)
    s_i = np.arange(128)[:, None]
    q_i = np.arange(128)[None, :]
    bc = np.zeros((128, 2, 4, 128), f32)
    bp = np.zeros((128, 2, 4, 128), f32)
    for kv in range(2):
        for gg in range(4):
            sl = slopes[kv * 4 + gg]
            pos = (gg % 2) * 2 + gg // 2
            bc[:, kv, pos, :] = np.where(q_i >= s_i, -8.0 * sl * (q_i - s_i), MASKV)
            bp[:, kv, pos, :] = np.where(s_i > q_i, -8.0 * sl * (q_i + 128 - s_i), MASKV)
    shared = dict(
        w_ffn1_up=g("w_ffn1_up"), w_ffn1_down=g("w_ffn1_down"), w_in=g("w_in"), w_mem_kv=g("w_mem_kv"),
        w_out=g("w_out"), w_ffn2_up=g("w_ffn2_up"), w_ffn2_down=g("w_ffn2_down"),
        gains=gains, convw=convw, sinks=sinks, ident=ident,
        bias_cur=np.ascontiguousarray(bc.reshape(128, 2, 512)), bias_prev=np.ascontiguousarray(bp.reshape(128, 2, 512)),
    )
    x = g("x")
    mem = g("mem")
    in_maps = []
    for i in range(x.shape[0]):
        m = dict(shared)
        m["x"] = np.ascontiguousarray(x[i])
        m["mem"] = np.ascontiguousarray(mem[i])
        in_maps.append(m)
    return in_maps


_NC_CACHE = {}


def kernel(**inputs):
    in_maps = host_layout(inputs)
    if "nc" not in _NC_CACHE:
        _NC_CACHE["nc"] = build_program()
    nc = _NC_CACHE["nc"]
    res = run_bass_kernel_spmd(nc, in_maps, core_ids=list(range(len(in_maps))))
    out = np.stack([np.asarray(r["out"], dtype=np.float32) for r in res.results], axis=0)
    return out
```

```python
import contextlib

import numpy as np
import concourse.bass as bass
import concourse.mybir as mybir
from concourse.bass_utils import run_bass_kernel_spmd

F32 = mybir.dt.float32
BF16 = mybir.dt.bfloat16
AF = mybir.ActivationFunctionType
ALU = mybir.AluOpType

PE, ACT, DVE, POOL, SP = "pe", "act", "dve", "pool", "sp"
GRAN = 256

D = 1024
SEQ = 4096
NT = 1024
NTILES = SEQ // NT
SUB = 512
NSUB = NT // SUB
NBLK = NT // 128
DFF = 2816
NFC = DFF // 128
NG = NFC // 2
DEPTH = 2
MEM = 256
EPS = 1e-6
MASKV = -240000.0

G_FFN1, G_MIX, G_MEM, G_GRP, G_FFN2, G_FINAL = 0, 2, 4, 6, 8, 10
MODE = {"mixer": "interleave", "c_split": (4, 2), "bank": "sim", "kouter": True, "kouter_wo": 4, "sq_dve": 4, "load_act": 2, "store_act": 2, "store_inplace": True, "half_den": True, "b_hold": 3, "early_fc": 12, "late_fin": False, "dconv_at": 6, "load_pipe": True, "pa_dve": (1, 3, 5), "early_a0": 5, "etab": False, "v_nodup": True, "sink_bias": True}


class _Op:
    __slots__ = ("eng", "fn", "deps", "is_dma", "dsem", "dval", "ord", "signal", "id", "t1", "t0", "crit", "tag")

    def __init__(self, eng, fn, is_dma):
        self.eng = eng
        self.fn = fn
        self.deps = {}
        self.is_dma = is_dma
        self.dsem = None
        self.dval = 0
        self.ord = 0
        self.signal = False
        self.id = -1
        self.t1 = 0.0


def _tracked(ap):
    return type(ap.tensor).__name__ in ("SBTensorHandle", "PSumTensorHandle")


def _region(ap):
    pat = ap.ap
    pstride = pat[0][0]
    off = ap.offset
    foff = off % pstride if pstride > 0 else off
    span = 1
    for st, cnt in pat[1:]:
        span += abs(st) * (cnt - 1)
    ds = mybir.dt.size(ap.dtype)
    b0, b1 = foff * ds, (foff + span) * ds
    if type(ap.tensor).__name__ == "PSumTensorHandle":
        b0 = (b0 // 2048) * 2048
        b1 = ((b1 + 2047) // 2048) * 2048
    return ap.name, b0, b1


def _nfree(ap):
    n = 1
    for _, cnt in ap.ap[1:]:
        n *= cnt
    return n


def _nbytes(ap):
    n = 1
    for _, cnt in ap.ap:
        n *= cnt
    return n * mybir.dt.size(ap.dtype)


class Prog:
    NDMA_SEMS = 14
    LAT = 0.15
    DMA_BPU = 300e3

    def __init__(self, nc):
        self.nc = nc
        self.ops = []
        self.last_w = {}
        self.readers = {}
        self.clock = {}
        self.unread = set()
        self.tag = ""
        self.bank_t = [0.0] * 8
        self.dma_free = 0.0

    def add(self, eng, fn, reads=(), writes=(), is_dma=False, cost=0.0):
        op = _Op(eng, fn, is_dma)
        op.id = len(self.ops)
        deps = op.deps
        if eng != PE:
            self._psum_check(reads)
        for ap in reads:
            if ap is None or not _tracked(ap):
                continue
            name, b0, b1 = _region(ap)
            for g in range(b0 // GRAN, (b1 - 1) // GRAN + 1):
                k = (name, g)
                w = self.last_w.get(k)
                if w is not None:
                    deps[w.id] = w
                self.readers.setdefault(k, []).append(op)
        for ap in writes:
            if ap is None or not _tracked(ap):
                continue
            name, b0, b1 = _region(ap)
            for g in range(b0 // GRAN, (b1 - 1) // GRAN + 1):
                k = (name, g)
                w = self.last_w.get(k)
                if w is not None and (w.eng != eng or w.is_dma or is_dma or eng != PE):
                    deps[w.id] = w
                for r in self.readers.get(k, ()):
                    if r is not op and (r.eng != eng or r.is_dma or is_dma or eng != PE):
                        deps[r.id] = r
                self.readers[k] = []
                self.last_w[k] = op
        deps.pop(op.id, None)
        t0 = self.clock.get(eng, 0.0)
        op.crit = None
        op.tag = self.tag
        for d in deps.values():
            if d.t1 + self.LAT > t0:
                t0 = d.t1 + self.LAT
                op.crit = d
        op.t0 = t0
        if is_dma:
            self.clock[eng] = t0 + (1.2 if eng == POOL else 0.06)
            ts = max(t0, self.dma_free)
            nb = max(_nbytes(ap) for ap in list(reads) + list(writes) if ap is not None)
            self.dma_free = ts + nb / self.DMA_BPU
            op.t1 = self.dma_free + 2.0
        else:
            op.t1 = t0 + cost
            self.clock[eng] = op.t1
        for ap in list(reads) + list(writes):
            if ap is not None and type(ap.tensor).__name__ == "PSumTensorHandle":
                _, b0, b1 = _region(ap)
                for b in range(b0 // 2048, (b1 - 1) // 2048 + 1):
                    self.bank_t[b] = op.t1 if self.bank_t[b] == float("inf") else max(self.bank_t[b], op.t1)
        self.ops.append(op)
        return op

    def emit(self, es):
        nc = self.nc
        engs = {PE: nc.tensor, ACT: nc.scalar, DVE: nc.vector, POOL: nc.gpsimd, SP: nc.sync}
        esem = {e: es.enter_context(nc.semaphore("sem_" + e)) for e in engs}
        nd = self.NDMA_SEMS
        dsems = [es.enter_context(nc.semaphore(f"sem_dma{i}")) for i in range(nd)]
        dcount = [0] * nd
        dlast = [None] * nd
        pools = {POOL: list(range(0, nd - 4)), SP: list(range(nd - 4, nd)), ACT: list(range(nd - 4, nd))}
        rrs = {POOL: 0, SP: 0, ACT: 0}

        seq = {}
        scount = {e: 0 for e in engs}
        for op in self.ops:
            if not op.is_dma:
                scount[op.eng] += 1
                seq[op.id] = scount[op.eng]
        wseq = {e: {} for e in engs}
        waited_dma = {e: set() for e in engs}
        ecount = {e: 0 for e in engs}
        plan = []
        for op in self.ops:
            waits = []
            if op.is_dma:
                pl = pools[op.eng]
                j = pl[rrs[op.eng] % len(pl)]
                rrs[op.eng] += 1
                prev = dlast[j]
                if prev is not None and prev.id not in waited_dma[op.eng]:
                    waits.append(("d", prev))
                    waited_dma[op.eng].add(prev.id)
                dcount[j] += 16
                op.dsem = j
                op.dval = dcount[j]
                dlast[j] = op
            best = {}
            for d in op.deps.values():
                if d.is_dma:
                    if d.id in waited_dma[op.eng]:
                        continue
                    waited_dma[op.eng].add(d.id)
                    waits.append(("d", d))
                else:
                    if d.eng == op.eng and op.eng == PE:
                        continue
                    b = best.get(d.eng)
                    if b is None or seq[d.id] > seq[b.id]:
                        best[d.eng] = d
            for pe_, d in best.items():
                if seq[d.id] > wseq[op.eng].get(pe_, 0):
                    wseq[op.eng][pe_] = seq[d.id]
                    waits.append(("e", d))
            plan.append(waits)
        need = set()
        for waits in plan:
            for kind, d in waits:
                if kind == "e":
                    need.add(d.id)
        for op in self.ops:
            if not op.is_dma and op.id in need:
                ecount[op.eng] += 1
                op.ord = ecount[op.eng]
                op.signal = True
        nwaits = 0
        for op, waits in zip(self.ops, plan):
            e = engs[op.eng]
            for kind, d in waits:
                if kind == "d":
                    e.wait_ge(dsems[d.dsem], d.dval)
                else:
                    e.wait_ge(esem[d.eng], d.ord)
                nwaits += 1
            ins = op.fn(e)
            if op.is_dma:
                ins.then_inc(dsems[op.dsem], 16)
            elif op.signal:
                ins.then_inc(esem[op.eng], 1)
        for j in range(nd):
            if dcount[j]:
                nc.sync.wait_ge(dsems[j], dcount[j])
        for en in engs:
            if ecount[en]:
                nc.sync.wait_ge(esem[en], ecount[en])
        self.stats = dict(est_us=max(o.t1 for o in self.ops), nops=len(self.ops), nwaits=nwaits, signals=dict(ecount), per_eng=dict(scount))

    @staticmethod
    def _c_act(ap):
        return 0.10 + 0.00098 * _nfree(ap)

    @staticmethod
    def _c_dve(out, *ins):
        n = _nfree(out)
        if all(mybir.dt.size(a.dtype) == 2 for a in (out,) + ins):
            return 0.08 + 0.0006 * n
        return 0.08 + 0.0012 * n

    def _raw_granules(self, ap):
        pat = ap.ap
        pstride = pat[0][0]
        foff = ap.offset % pstride if pstride > 0 else ap.offset
        span = 1
        for st, cnt in pat[1:]:
            span += abs(st) * (cnt - 1)
        ds = mybir.dt.size(ap.dtype)
        return range(foff * ds // GRAN, ((foff + span) * ds - 1) // GRAN + 1)

    def _psum_check(self, reads, fresh_write=None):
        for ap in reads:
            if ap is not None and type(ap.tensor).__name__ == "PSumTensorHandle":
                for g in self._raw_granules(ap):
                    self.unread.discard(g)
        if fresh_write is not None:
            for g in self._raw_granules(fresh_write):
                assert g not in self.unread, ("PSUM tile overwritten before it was read", g * GRAN // 2048)
                self.unread.add(g)

    def mm(self, out, lhsT, rhs, start=True, stop=True):
        self._psum_check([], out if start else None)
        n = _nfree(rhs)
        cost = max(n / 2400.0, 0.055) + 0.003
        return self.add(PE, lambda e: e.matmul(out, lhsT=lhsT, rhs=rhs, start=start, stop=stop),
                        reads=[lhsT, rhs], writes=[out], cost=cost)

    def transpose(self, out, in_, ident):
        self._psum_check([], out)
        return self.add(PE, lambda e: e.transpose(out, in_, ident), reads=[in_, ident], writes=[out], cost=0.12)

    def act(self, out, in_, func, bias=None, scale=None):
        kw = {}
        if bias is not None:
            kw["bias"] = bias
        if scale is not None:
            kw["scale"] = scale
        rd = [in_] + ([bias] if (bias is not None and not isinstance(bias, (int, float))) else [])
        return self.add(ACT, lambda e: e.activation(out=out, in_=in_, func=func, **kw), reads=rd, writes=[out],
                        cost=self._c_act(out))

    def copy(self, out, in_, eng=DVE):
        if eng == ACT:
            return self.add(ACT, lambda e: e.copy(out=out, in_=in_), reads=[in_], writes=[out], cost=self._c_act(out))
        return self.add(eng, lambda e: e.tensor_copy(out=out, in_=in_), reads=[in_], writes=[out],
                        cost=self._c_dve(out, in_))

    def tt(self, out, in0, in1, op, eng=DVE):
        return self.add(eng, lambda e: e.tensor_tensor(out=out, in0=in0, in1=in1, op=op),
                        reads=[in0, in1], writes=[out], cost=self._c_dve(out, in0, in1))

    def ts1(self, out, in0, s1, op0, eng=DVE):
        rd = [in0] + ([] if isinstance(s1, (int, float)) else [s1])
        return self.add(eng, lambda e: e.tensor_scalar(out=out, in0=in0, scalar1=s1, scalar2=None, op0=op0),
                        reads=rd, writes=[out], cost=self._c_dve(out, in0))

    def stt(self, out, in0, scalar, in1, op0, op1, eng=DVE):
        rd = [in0, in1] + ([] if isinstance(scalar, (int, float)) else [scalar])
        return self.add(eng, lambda e: e.scalar_tensor_tensor(out=out, in0=in0, scalar=scalar, in1=in1,
                                                              op0=op0, op1=op1),
                        reads=rd, writes=[out], cost=self._c_dve(out, in0, in1))

    def recip(self, out, in_):
        return self.add(DVE, lambda e: e.reciprocal(out=out, in_=in_), reads=[in_], writes=[out], cost=3.0)

    def memset(self, ap, val, eng=DVE):
        return self.add(eng, lambda e: e.memset(ap, val), reads=[], writes=[ap], cost=0.08 + 0.0006 * _nfree(ap))

    def dma(self, out, in_, eng=SP):
        return self.add(eng, lambda e: e.dma_start(out=out, in_=in_), reads=[in_], writes=[out], is_dma=True)


class WStream:
    def __init__(self, slots, loads):
        self.slots = slots
        self.loads = loads
        self.R = len(slots)
        self.issued = 0
        self.head = 0
        self.freed = 0
        self.other = None

    def _pump(self):
        lim = min(self.freed + self.R, len(self.loads))
        while self.issued < lim:
            i = self.issued
            self.loads[i][1](self.slots[i % self.R])
            self.issued += 1

    def acquire(self, key):
        i = self.head
        assert self.loads[i][0] == key, (self.loads[i][0], key)
        self._pump()
        assert self.issued > i
        self.head += 1
        return self.slots[i % self.R]

    def release(self, n=1):
        self.freed += n
        self._pump()
        if self.other is not None:
            self.other._pump()


def build_program(ntiles=NTILES, depth=DEPTH, raw_out=False, stop_after=None, dbg=()):
    nc = bass.Bass("TRN2", target_bir_lowering=False)
    P = Prog(nc)

    def din(name, shape):
        return nc.dram_tensor(name, list(shape), F32, kind="ExternalInput").ap()

    x_d = din("x", [SEQ, D])
    mem_d = din("mem", [MEM, D])
    wup_d = {1: din("w_ffn1_up", [DEPTH, D, 2 * DFF]), 2: din("w_ffn2_up", [DEPTH, D, 2 * DFF])}
    wdn_d = {1: din("w_ffn1_down", [DEPTH, DFF, D]), 2: din("w_ffn2_down", [DEPTH, DFF, D])}
    win_d = din("w_in", [DEPTH, D, 1792])
    wmem_d = din("w_mem_kv", [DEPTH, D, 512])
    wout_d = din("w_out", [DEPTH, D, D])
    gains_d = din("gains", [128, 88])
    convw_d = din("convw", [128, 12])
    sinks_d = din("sinks", [1, 16])
    ident_d = din("ident", [128, 128])
    biasc_d = din("bias_cur", [128, 2, 512])
    biasp_d = din("bias_prev", [128, 2, 512])
    out_d = nc.dram_tensor("out", [SEQ, D], F32, kind="ExternalOutput").ap()

    with contextlib.ExitStack() as es:
        def sb(name, shape, dt=F32):
            return es.enter_context(nc.sbuf_tensor(name, list(shape), dt))

        ps = es.enter_context(nc.psum_tensor("ps", [128, 8, 512], F32))
        bank_ctr = [0]
        reserved = set()

        pool_ctr = {}
        ATT = None
        LIN = None
        lin_pool = [None]

        def bank(pool=None):
            if pool is None and MODE["bank"] == "sim":
                now = P.clock.get(PE, 0.0)
                best = None
                for k in range(8):
                    b = (bank_ctr[0] + k) % 8
                    if b in reserved:
                        continue
                    key = max(P.bank_t[b], now)
                    if best is None or key < best[0] - 1e-9:
                        best = (key, b, k)
                bank_ctr[0] += best[2] + 1
                P.bank_t[best[1]] = float("inf")
                return best[1]
            while True:
                if pool is None:
                    b = bank_ctr[0] % 8
                    bank_ctr[0] += 1
                else:
                    name, cand = pool
                    i = pool_ctr.get(name, 0)
                    pool_ctr[name] = i + 1
                    b = cand[i % len(cand)]
                if b not in reserved:
                    return b

        def bank_reserve(pool=None):
            b = bank(pool)
            reserved.add(b)
            return b

        hT = sb("hT", [128, 8, NT])
        xn = sb("xn", [128, 8, NT], BF16)
        big = sb("big", [128, 11264])
        bigb = big[:].bitcast(BF16)
        hid = bigb.rearrange("p (c n) -> p c n", c=NFC)
        qT = bigb[:, 0:4096].rearrange("p (c n) -> p c n", c=4)
        kT = bigb[:, 4096:8192].rearrange("p (c n) -> p c n", c=4)
        qmT = bigb[:, 8192:10240].rearrange("p (c n) -> p c n", c=2)
        Vt = bigb[:, 10240:12288].rearrange("p (b n) -> p b n", b=8)
        ybuf = big[:, 6144:10240].rearrange("p (c n) -> p c n", c=8)
        onorm = big[:, 0:4096].rearrange("p (c n) -> p c n", c=8)
        ostage = big[:, 4096:6144].rearrange("p (b n) -> p b n", b=2)
        NXS = 8
        xbig = big[:, 6144:11264].rearrange("p (b n) -> p b n", b=5)
        xnf = xn[:].bitcast(F32).rearrange("p c n -> p (c n)").rearrange("p (b n) -> p b n", b=4)
        xstage_l = [xnf[:, i, :] for i in range(4)] + [xbig[:, i, :] for i in range(4)]
        mstage_l = [xbig[:, 4, :], ostage[:, 0, :]]
        memT = big[:, 0:2048].rearrange("p (c n) -> p c n", c=8)
        memn = bigb[:, 4096:6144].rearrange("p (c n) -> p c n", c=8)

        NWU, NWD = 4, 5
        wu_slots = [sb(f"wu{i}", [128, 8, 512], BF16) for i in range(NWU)]
        wd_slots = [sb(f"wd{i}", [128, NFC, 128], BF16) for i in range(NWD)]

        kcar = [sb(f"kcar{l}", [128, 4, 128], BF16) for l in range(DEPTH)]
        vcar = [sb(f"vcar{l}", [128, 256], BF16) for l in range(DEPTH)]
        ccar = [[sb(f"ccar{l}_{j}", [128, 2]) for j in range(2)] for l in range(DEPTH)]
        MKz = [sb(f"mkz{l}", [128, 4, MEM], BF16) for l in range(DEPTH)]
        MV = [sb(f"mv{l}", [128, 2, 256], BF16) for l in range(DEPTH)]
        biasC = sb("biasC", [128, 2, 512], BF16)
        biasP = sb("biasP", [128, 2, 512], BF16)
        SR = sb("SR", [128, 4, 512], BF16)
        identf = sb("identf", [128, 128])
        identb = sb("identb", [128, 128], BF16)
        ones_m = {n: sb(f"ones{n}", [128, 128], BF16) for n in (1024, 512, 256, 1)}
        sel = sb("sel", [128, 128], BF16)
        ones_h = [sb(f"ones_h{i}", [128, 128], BF16) for i in range(2)]
        sel_h = [sb(f"sel_h{i}", [128, 128], BF16) for i in range(2)]
        RS = sb("RS", [128, 2, 128], BF16)
        SKB = sb("SKB", [128, 128])
        Gt = sb("Gt", [128, 88])
        CW = sb("CW", [128, 12])
        sk = sb("sk", [64, 16]); ske = sb("ske", [64, 16]); skh = sb("skh", [64, 16], BF16)
        skhf = sb("skhf", [64, 16]); skl = sb("skl", [64, 16])
        Pt = [[sb(f"P{i}_{j}", [128, 512], BF16) for j in range(2)] for i in range(2)]
        rec = [sb(f"rec{i}", [128, 256 if MODE["half_den"] else 512]) for i in range(2)]
        sg = [sb(f"sg{i}", [128, 512]) for i in range(2)]
        NSQ = 8
        sq = [sb(f"sq{i}", [128, 512], BF16) for i in range(NSQ)]
        sq_busy = [False] * NSQ
        sd = [sb(f"sd{i}", [128, 512]) for i in range(2)]
        lnd = sd
        rstd = [sb(f"rstd{i}", [128, 512]) for i in range(2)]
        vbuf = [sb(f"vbuf{j}", [128, 514]) for j in range(2)]
        csb = sb("csb", [128, 512])
        cacc = sb("cacc", [128, 512])

        ctr = {"sq": 0, "nrm": 0, "sg": 0, "ev": 0}

        def sq_get(n):
            for k in range(NSQ):
                i = (ctr["sq"] + k) % NSQ
                if not sq_busy[i]:
                    break
            else:
                raise AssertionError("no free square buffer")
            ctr["sq"] = i + 1
            sq_busy[i] = True
            return i, sq[i][:, 0:n]

        def evac(out, in_, act_of4=2):
            ctr["ev"] += 1
            pat = {0: (), 1: (1,), 2: (1, 3), 3: (0, 1, 3), 4: (0, 1, 2, 3)}[act_of4]
            P.copy(out, in_, eng=ACT if ctr["ev"] % 4 in pat else DVE)

        def v_k(w, l):
            return w[l].rearrange("(kc p) f -> p kc f", p=128)

        def ld_up(which, l, g):
            def f(slot):
                v = v_k(wup_d[which], l)
                P.dma(slot[:, :, 0:256], v[:, :, g * 256:(g + 1) * 256], eng=POOL)
                P.dma(slot[:, :, 256:512], v[:, :, DFF + g * 256:DFF + (g + 1) * 256], eng=POOL)
            return f

        def ld_in(l, i):
            def f(slot):
                v = v_k(win_d, l)
                if i == 0:
                    P.dma(slot[:, :, :], v[:, :, 0:512], eng=POOL)
                elif i == 1:
                    P.dma(slot[:, :, :], v[:, :, 512:1024], eng=POOL)
                else:
                    if i == 2:
                        P.dma(slot[:, :, 0:256], v[:, :, 1024:1280], eng=POOL)
                        base = 1280
                    else:
                        P.dma(slot[:, :, 0:256], v[:, :, 1536:1792], eng=POOL)
                        base = 1408
                    if i == 3 and MODE["v_nodup"]:
                        P.dma(slot[:, :, 256:384], v[:, :, base:base + 128], eng=POOL)
                    else:
                        for kv in range(2):
                            for h in range(2):
                                c0 = 256 + kv * 128 + h * 64
                                P.dma(slot[:, :, c0:c0 + 64], v[:, :, base + kv * 64:base + kv * 64 + 64], eng=POOL)
            return f

        def ld_out(l, i):
            def f(slot):
                P.dma(slot[:, :, :], v_k(wout_d, l)[:, :, i * 512:(i + 1) * 512], eng=POOL)
            return f

        def ld_mem(l):
            def f(slot):
                P.dma(slot[:, :, :], v_k(wmem_d, l)[:, :, :], eng=POOL)
            return f

        def ld_down(which, l, dc):
            def f(slot):
                P.dma(slot[:, :, :], v_k(wdn_d[which], l)[:, :, dc * 128:(dc + 1) * 128], eng=POOL)
            return f

        wu_loads = [(("mem", l), ld_mem(l)) for l in range(depth)]
        wd_loads = []
        for t in range(ntiles):
            for l in range(depth):
                wu_loads += [(("up", 1, l, g), ld_up(1, l, g)) for g in range(NG)]
                wd_loads += [(("down", 1, l, dc), ld_down(1, l, dc)) for dc in range(8)]
                wu_loads += [(("in", l, i), ld_in(l, i)) for i in range(4)]
                wu_loads += [(("out", l, i), ld_out(l, i)) for i in range(2)]
                wu_loads += [(("up", 2, l, g), ld_up(2, l, g)) for g in range(NG)]
                wd_loads += [(("down", 2, l, dc), ld_down(2, l, dc)) for dc in range(8)]
        WU = WStream(wu_slots, wu_loads)
        WD = WStream(wd_slots, wd_loads)

        def norm_split(srcs, gidx, gc0, nfeat, dsts, n):
            sqs = []
            for s_ in srcs:
                i, q = sq_get(n)
                P.act(q, s_, AF.Square)
                sqs.append((i, q))

            def part_b():
                i0 = ctr["nrm"] % 2
                ctr["nrm"] += 1
                b = bank(lin_pool[0])
                pss = ps[:, b, 0:n]
                last = len(sqs) - 1
                for k, (i, q) in enumerate(sqs):
                    P.mm(pss, ones_m[nfeat][:], q, start=(k == 0), stop=(k == last))
                    sq_busy[i] = False
                P.act(sd[i0][:, 0:n], pss, AF.Ln, bias=EPS, scale=1.0)
                P.act(rstd[i0][:, 0:n], sd[i0][:, 0:n], AF.Exp, scale=-0.5)
                for k, (s_, d_) in enumerate(zip(srcs, dsts)):
                    gcol = Gt[:, gidx * 8 + gc0 + k:gidx * 8 + gc0 + k + 1]
                    P.stt(d_, s_, gcol, rstd[i0][:, 0:n], ALU.mult, ALU.mult)
            return part_b

        def norm(srcs, gidx, gc0, nfeat, dsts, n):
            norm_split(srcs, gidx, gc0, nfeat, dsts, n)()

        class NormAcc:
            def __init__(self, s):
                self.s = s
                self.b = bank_reserve(lin_pool[0])
                self.cnt = 0
                self.half = None
                self.pending = []
                self.i0 = None

            def add(self, src, on_dve=False):
                i, q = sq_get(SUB)
                if on_dve:
                    P.tt(q, src, src, ALU.mult)
                else:
                    P.act(q, src, AF.Square)
                if self.half is None:
                    self.half = (i, q)
                else:
                    ia, qa = self.half
                    P.tt(qa, qa, q, ALU.add)
                    sq_busy[i] = False
                    self.half = None
                    self.pending.append((ia, qa))

            def flush(self):
                for i, q in self.pending:
                    P.mm(ps[:, self.b, :], ones_m[1024][:], q, start=(self.cnt == 0), stop=(self.cnt == 3))
                    sq_busy[i] = False
                    self.cnt += 1
                self.pending = []

            def finish_a(self):
                if self.i0 is not None:
                    return
                self.flush()
                assert self.cnt == 4 and self.half is None
                i0 = ctr["nrm"] % 2
                ctr["nrm"] += 1
                self.i0 = i0
                P.act(sd[i0][:], ps[:, self.b, :], AF.Ln, bias=EPS, scale=1.0)
                P.act(rstd[i0][:], sd[i0][:], AF.Exp, scale=-0.5)
                reserved.discard(self.b)

            def finish(self, gidx, dsts):
                self.finish_a()
                i0 = self.i0
                ssl = slice(self.s * SUB, (self.s + 1) * SUB)
                for c in range(8):
                    gcol = Gt[:, gidx * 8 + c:gidx * 8 + c + 1]
                    P.stt(dsts[c], hT[:, c, ssl], gcol, rstd[i0][:], ALU.mult, ALU.mult,
                          eng=(POOL if ("poolnorm" in dbg and c % 2 == 1) else DVE))

        def plain_accs():
            accs = [NormAcc(s) for s in range(NSUB)]
            for s in range(NSUB):
                for c in range(8):
                    accs[s].add(hT[:, c, s * SUB:(s + 1) * SUB])
                    accs[s].flush()
            return accs

        def ffn(which, l, accs, nxt=None):
            gidx = (G_FFN1 if which == 1 else G_FFN2) + l
            P.tag = "ffn-up"
            late = MODE["late_fin"] and MODE["kouter"]
            for s in range(NSUB):
                ssl = slice(s * SUB, (s + 1) * SUB)
                if s == 0 or not late:
                    accs[s].finish(gidx, [xn[:, c, ssl] for c in range(8)])
            for g in range(NG):
                w = WU.acquire(("up", which, l, g))
                for s in range(NSUB):
                    ssl = slice(s * SUB, (s + 1) * SUB)
                    if g == 0 and s == 1 and late:
                        accs[1].finish(gidx, [xn[:, c, ssl] for c in range(8)])
                    if g == 0 and MODE["kouter"]:
                        bks = [(bank(), bank()) for j in range(2)]
                        for k in range(8):
                            for j in range(2):
                                P.mm(ps[:, bks[j][0], :], w[:, k, j * 128:(j + 1) * 128], xn[:, k, ssl],
                                     start=(k == 0), stop=(k == 7))
                                P.mm(ps[:, bks[j][1], :], w[:, k, 256 + j * 128:256 + (j + 1) * 128], xn[:, k, ssl],
                                     start=(k == 0), stop=(k == 7))
                        for j in range(2):
                            sgb = sg[ctr["sg"] % 2]
                            ctr["sg"] += 1
                            P.act(sgb[:], ps[:, bks[j][0], :], AF.Silu)
                            P.tt(hid[:, 2 * g + j, ssl], sgb[:], ps[:, bks[j][1], :], ALU.mult)
                        continue
                    for j in range(2):
                        fc = 2 * g + j
                        bg, bu = bank(), bank()
                        for k in range(8):
                            P.mm(ps[:, bg, :], w[:, k, j * 128:(j + 1) * 128], xn[:, k, ssl], start=(k == 0), stop=(k == 7))
                        for k in range(8):
                            P.mm(ps[:, bu, :], w[:, k, 256 + j * 128:256 + (j + 1) * 128], xn[:, k, ssl],
                                 start=(k == 0), stop=(k == 7))
                        sgb = sg[ctr["sg"] % 2]
                        ctr["sg"] += 1
                        P.act(sgb[:], ps[:, bg, :], AF.Silu)
                        P.tt(hid[:, fc, ssl], sgb[:], ps[:, bu, :], ALU.mult)
                WU.release()
            nacc = [NormAcc(s) for s in range(NSUB)]
            pend = []
            P.tag = "ffn-down"
            for dc in range(8):
                w = WD.acquire(("down", which, l, dc))
                for s in range(NSUB):
                    ssl = slice(s * SUB, (s + 1) * SUB)
                    b = bank()
                    for fc in range(NFC):
                        P.mm(ps[:, b, :], w[:, fc, :], hid[:, fc, ssl], start=(fc == 0), stop=(fc == NFC - 1))
                        if (nxt is not None and dc == 7 and s == NSUB - 1 and MODE["early_fc"]
                                and fc == MODE["early_fc"] - 1):
                            nacc[0].finish_a()
                    if len(pend) >= 2:
                        nacc[pend.pop(0)].flush()
                    P.stt(hT[:, dc, ssl], ps[:, b, :], 0.5, hT[:, dc, ssl], ALU.mult, ALU.add)
                    nacc[s].add(hT[:, dc, ssl])
                    pend.append(s)
                WD.release()
            return nacc

        x_issued = {}

        def x_dma(t, upto):
            i = x_issued.get(t, 0)
            while i < min(upto, NBLK):
                P.dma(xstage_l[i % NXS], x_d[t * NT + i * 128:t * NT + (i + 1) * 128, :], eng=SP)
                i += 1
            x_issued[t] = i

        def load_x_accs(t):
            P.tag = "load"
            accs = []
            bps = NBLK // NSUB
            for b in range(NBLK):
                x_dma(t, b + NXS)
                st = xstage_l[b % NXS]
                for half in range(2):
                    bk = bank()
                    for cc in range(4):
                        c = half * 4 + cc
                        P.transpose(ps[:, bk, cc * 128:(cc + 1) * 128], st[:, c * 128:(c + 1) * 128], identf[:])
                    src = ps[:, bk, :].rearrange("p (c n) -> p c n", c=4)
                    evac(hT[:, half * 4:half * 4 + 4, b * 128:(b + 1) * 128], src, MODE["load_act"])
                if b % bps == bps - 1:
                    s = b // bps
                    acc = NormAcc(s)
                    for c in range(8):
                        acc.add(hT[:, c, s * SUB:(s + 1) * SUB], on_dve=(c in MODE["pa_dve"]))
                        acc.flush()
                    accs.append(acc)
            return accs

        def load_x(t):
            P.tag = "load"
            for b in range(NBLK):
                x_dma(t, b + NXS)
                st = xstage_l[b % NXS]
                for half in range(2):
                    bk = bank()
                    for cc in range(4):
                        c = half * 4 + cc
                        P.transpose(ps[:, bk, cc * 128:(cc + 1) * 128], st[:, c * 128:(c + 1) * 128], identf[:])
                    src = ps[:, bk, :].rearrange("p (c n) -> p c n", c=4)
                    evac(hT[:, half * 4:half * 4 + 4, b * 128:(b + 1) * 128], src, MODE["load_act"])

        ostage6 = big[:, 0:6144].rearrange("p (b n) -> p b n", b=6)

        def store_out_inplace(t, accs):
            P.tag = "store"
            for s in range(NSUB):
                ssl = slice(s * SUB, (s + 1) * SUB)
                accs[s].finish(G_FINAL, [hT[:, c, ssl] for c in range(8)])
            for s in range(NSUB):
                for bb in range(4):
                    i = s * 4 + bb
                    ost = ostage6[:, i % 6, :]
                    c0 = s * SUB + bb * 128
                    for half in range(2):
                        bk = bank()
                        for cc in range(4):
                            c = half * 4 + cc
                            P.transpose(ps[:, bk, cc * 128:(cc + 1) * 128], hT[:, c, c0:c0 + 128], identf[:])
                        evac(ost[:, half * 512:(half + 1) * 512], ps[:, bk, :], MODE["store_act"])
                    r0 = t * NT + c0
                    P.dma(out_d[r0:r0 + 128, :], ost, eng=SP)

        def store_out(t, accs):
            if MODE["store_inplace"] and not raw_out:
                return store_out_inplace(t, accs)
            P.tag = "store"
            for s in range(NSUB):
                ssl = slice(s * SUB, (s + 1) * SUB)
                if raw_out:
                    accs[s].flush()
                    reserved.discard(accs[s].b)
                    srcs = [hT[:, c, ssl] for c in range(8)]
                else:
                    accs[s].finish(G_FINAL, [onorm[:, c, :] for c in range(8)])
                    srcs = [onorm[:, c, :] for c in range(8)]
                for bb in range(4):
                    ost = ostage[:, bb % 2, :]
                    for half in range(2):
                        bk = bank()
                        for cc in range(4):
                            c = half * 4 + cc
                            P.transpose(ps[:, bk, cc * 128:(cc + 1) * 128], srcs[c][:, bb * 128:(bb + 1) * 128], identf[:])
                        evac(ost[:, half * 512:(half + 1) * 512], ps[:, bk, :], MODE["store_act"])
                    r0 = t * NT + s * SUB + bb * 128
                    P.dma(out_d[r0:r0 + 128, :], ost, eng=SP)

        def mixer(l, t, accs):
            P.tag = "mix-A"
            late = MODE["late_fin"] and MODE["kouter"] and MODE["mixer"] != "seq"
            for s in range(NSUB):
                ssl = slice(s * SUB, (s + 1) * SUB)
                if s == 0 or not late:
                    accs[s].finish(G_MIX + l, [xn[:, c, ssl] for c in range(8)])
            for kv in range(2):
                P.memset(kT[64:128, kv * 2 + 0, :], 0.0)
                P.memset(kT[0:64, kv * 2 + 1, :], 0.0)
            dconv = [None] * NSUB
            dgrp = [None] * NSUB
            w0 = WU.acquire(("in", l, 0))
            w1 = WU.acquire(("in", l, 1))
            w2 = WU.acquire(("in", l, 2))
            w3 = WU.acquire(("in", l, 3))

            def proj(out_bank_ap, wslot, c0, rhs_of_k, n=128):
                for k in range(8):
                    P.mm(out_bank_ap, wslot[:, k, c0:c0 + n], rhs_of_k(k), start=(k == 0), stop=(k == 7))

            def proj_gen(s):
                ssl = slice(s * SUB, (s + 1) * SUB)
                xs = lambda k: xn[:, k, ssl]
                last = (s == NSUB - 1)
                for j in range(2):
                    pC, pX, pB = bank(lin_pool[0]), bank(lin_pool[0]), bank(lin_pool[0])
                    reserved.update((pC, pX, pB))
                    if s == 0 and j == 0 and MODE["kouter"]:
                        for k in range(8):
                            P.mm(ps[:, pC, :], w0[:, k, 256:384], xs(k), start=(k == 0), stop=(k == 7))
                            P.mm(ps[:, pX, :], w1[:, k, 0:128], xs(k), start=(k == 0), stop=(k == 7))
                            P.mm(ps[:, pB, :], w0[:, k, 0:128], xs(k), start=(k == 0), stop=(k == 7))
                        yield
                        yield
                    else:
                        proj(ps[:, pC, :], w0, 256 + j * 128, xs)
                        yield
                        proj(ps[:, pX, :], w1, j * 128, xs)
                        yield
                        proj(ps[:, pB, :], w0, j * 128, xs)
                    if j == 0 and s > 0 and dconv[s - 1] is not None:
                        dconv[s - 1]()
                        dconv[s - 1] = None
                    vb = vbuf[j]
                    cw = CW[:, (l * 2 + j) * 3:(l * 2 + j) * 3 + 3]
                    P.copy(csb[:], ps[:, pC, :], eng=ACT)
                    P.copy(vb[:, 0:2], ccar[l][j][:], eng=DVE)
                    P.tt(vb[:, 2:514], csb[:], ps[:, pX, :], ALU.mult)
                    P.ts1(cacc[:], vb[:, 2:514], cw[:, 2:3], ALU.mult)
                    P.stt(cacc[:], vb[:, 1:513], cw[:, 1:2], cacc[:], ALU.mult, ALU.add)
                    P.stt(cacc[:], vb[:, 0:512], cw[:, 0:1], cacc[:], ALU.mult, ALU.add)
                    P.tt(ybuf[:, j, :], cacc[:], ps[:, pB, :], ALU.mult)
                    P.copy(ccar[l][j][:], vb[:, 512:514], eng=DVE)
                    for b_ in (pC, pX, pB):
                        reserved.discard(b_)
                    yield
                if last:
                    WU.release(1)
                for c in range(4):
                    b = bank(lin_pool[0])
                    if c < 2:
                        proj(ps[:, b, :], w1, 256 + c * 128, xs)
                    else:
                        proj(ps[:, b, :], w2, (c - 2) * 128, xs)
                    evac(qT[:, c, ssl], ps[:, b, :])
                    if c == 1 and last:
                        WU.release(1)
                    yield
                for kv in range(2):
                    b = bank(lin_pool[0])
                    proj(ps[:, b, :], w2, 256 + kv * 128, xs)
                    P.copy(kT[0:64, kv * 2 + 0, ssl], ps[0:64, b, :], eng=ACT)
                    P.copy(kT[64:128, kv * 2 + 1, ssl], ps[64:128, b, :], eng=DVE)
                    yield
                if last:
                    WU.release(1)
                for c in range(2):
                    b = bank(lin_pool[0])
                    proj(ps[:, b, :], w3, c * 128, xs)
                    evac(qmT[:, c, ssl], ps[:, b, :])
                    yield
                for bl in range(4):
                    blk = s * 4 + bl
                    b = bank(lin_pool[0])
                    if MODE["v_nodup"]:
                        for k in range(8):
                            P.mm(ps[:, b, 0:128], xn[:, k, blk * 128:(blk + 1) * 128], w3[:, k, 256:384],
                                 start=(k == 0), stop=(k == 7))
                        vsrc = ps[:, b, 0:128].rearrange("p (kv d) -> p kv d", kv=2).unsqueeze(2)
                        evac(Vt[:, blk, :].rearrange("p (kv h d) -> p kv h d", kv=2, h=2),
                             vsrc.to_broadcast([128, 2, 2, 64]))
                    else:
                        for k in range(8):
                            P.mm(ps[:, b, 0:256], xn[:, k, blk * 128:(blk + 1) * 128], w3[:, k, 256:512],
                                 start=(k == 0), stop=(k == 7))
                        evac(Vt[:, blk, :], ps[:, b, 0:256])
                    yield
                if last:
                    WU.release(1)
                dconv[s] = norm_split([ybuf[:, 0, :], ybuf[:, 1, :]], G_GRP + l, 0, 256,
                                      [xn[:, 0, ssl], xn[:, 1, ssl]], SUB)

            def kblk(v, blk):
                return kT[:, v, blk * 128:(blk + 1) * 128] if blk >= 0 else kcar[l][:, v, :]

            def vblk(kv, blk):
                return Vt[:, blk, kv * 128:(kv + 1) * 128] if blk >= 0 else vcar[l][:, kv * 128:(kv + 1) * 128]

            def swa1(u):
                blk, kv, pi = u["blk"], u["kv"], u["pi"]
                qsl = slice(blk * 128, (blk + 1) * 128)
                has_prev = (t * NBLK + blk) > 0
                u["has_prev"] = has_prev
                et = MODE["etab"]
                bc = bank(ATT)
                if not et:
                    P.mm(ps[:, bc, :], identb[:], biasC[:, kv, :], start=True, stop=False)
                for par in range(2):
                    P.mm(ps[:, bc, par * 256:(par + 1) * 256].rearrange("p (j q) -> p j q", j=2),
                         kblk(kv * 2 + par, blk), qT[:, kv * 2:kv * 2 + 2, qsl],
                         start=bool(et), stop=(True if et else par == 1))
                P.act(Pt[pi][0][:], ps[:, bc, :], AF.Exp, scale=0.125)
                if et:
                    P.tt(Pt[pi][0][:], Pt[pi][0][:], biasC[:, kv, :], ALU.mult)
                if has_prev:
                    bp = bank(ATT)
                    if not et:
                        P.mm(ps[:, bp, :], identb[:], biasP[:, kv, :], start=True, stop=False)
                    for par in range(2):
                        P.mm(ps[:, bp, par * 256:(par + 1) * 256].rearrange("p (j q) -> p j q", j=2),
                             kblk(kv * 2 + par, blk - 1), qT[:, kv * 2:kv * 2 + 2, qsl],
                             start=bool(et), stop=(True if et else par == 1))
                    P.act(Pt[pi][1][:], ps[:, bp, :], AF.Exp, scale=0.125)
                    if et:
                        P.tt(Pt[pi][1][:], Pt[pi][1][:], biasP[:, kv, :], ALU.mult)

            def finish(u, bo, bd, c0, grouped=False, sink_col=None):
                pi = u["pi"]
                bl = u["blk"] % 4
                half = MODE["half_den"]
                nd = 256 if half else 512
                if sink_col is not None:
                    for j in range(2):
                        P.act(lnd[pi][:, j * 128:(j + 1) * 128], ps[:, bd, j * 128:(j + 1) * 128], AF.Ln,
                              bias=SKB[:, sink_col + j:sink_col + j + 1], scale=1.0)
                else:
                    P.act(lnd[pi][:, 0:nd], ps[:, bd, 0:nd], AF.Ln)
                P.act(rec[pi][:, 0:nd], lnd[pi][:, 0:nd], AF.Exp, scale=-1.0)
                for par in range(2):
                    p0 = par * 64
                    if grouped:
                        src = ps[p0:p0 + 64, bo, par * 256:(par + 1) * 256].rearrange("p (g q) -> p g q", g=2)
                    else:
                        src = ps[p0:p0 + 64, bo, :].rearrange("p (g q) -> p g q", g=4)[:, par::2, :]
                    if half:
                        rv = rec[pi][p0:p0 + 64, 0:256].rearrange("p (g q) -> p g q", g=2)
                    elif grouped:
                        rv = rec[pi][p0:p0 + 64, par * 256:(par + 1) * 256].rearrange("p (g q) -> p g q", g=2)
                    else:
                        rv = rec[pi][p0:p0 + 64, :].rearrange("p (g q) -> p g q", g=4)[:, par::2, :]
                    dst = ybuf[p0:p0 + 64, c0:c0 + 2, bl * 128:(bl + 1) * 128]
                    P.tt(dst, src, rv, ALU.mult)

            def swa2(u):
                blk, kv, pi = u["blk"], u["kv"], u["pi"]
                hp = u["has_prev"]
                bo, bd = bank(ATT), bank(ATT)
                P.mm(ps[:, bo, :], vblk(kv, blk), Pt[pi][0][:], start=True, stop=not hp)
                if hp:
                    P.mm(ps[:, bo, :], vblk(kv, blk - 1), Pt[pi][1][:], start=False, stop=True)
                if MODE["half_den"]:
                    dn = ps[:, bd, 0:256]
                    first = True
                    sb_ = MODE["sink_bias"]
                    pts = [Pt[pi][0]] + ([Pt[pi][1]] if hp else [])
                    for ip, pt in enumerate(pts):
                        for h in range(2):
                            P.mm(dn, ones_h[h][:], pt[:, h * 256:(h + 1) * 256], start=first,
                                 stop=(sb_ and ip == len(pts) - 1 and h == 1))
                            first = False
                    if not sb_:
                        for h in range(2):
                            P.mm(dn, sel_h[h][:], SR[:, l * 2 + kv, h * 256:(h + 1) * 256], start=False, stop=(h == 1))
                else:
                    P.mm(ps[:, bd, :], ones_m[1][:], Pt[pi][0][:], start=True, stop=False)
                    if hp:
                        P.mm(ps[:, bd, :], ones_m[1][:], Pt[pi][1][:], start=False, stop=False)
                    P.mm(ps[:, bd, :], sel[:], SR[:, l * 2 + kv, :], start=False, stop=True)
                finish(u, bo, bd, 2 + kv * 2, grouped=True,
                       sink_col=((l * 2 + kv) * 2 if (MODE["half_den"] and MODE["sink_bias"]) else None))

            def mem1(u):
                blk, pi = u["blk"], u["pi"]
                qsl = slice(blk * 128, (blk + 1) * 128)
                for mc in range(2):
                    b = bank(ATT)
                    for hm in range(4):
                        P.mm(ps[:, b, hm * 128:(hm + 1) * 128], MKz[l][:, hm, mc * 128:(mc + 1) * 128],
                             qmT[:, hm // 2, qsl], start=True, stop=True)
                    P.act(Pt[pi][mc][:], ps[:, b, :], AF.Exp, scale=0.125)

            def mem2(u):
                pi = u["pi"]
                bo, bd = bank(ATT), bank(ATT)
                for c in range(2):
                    for mc in range(2):
                        P.mm(ps[:, bo, c * 256:(c + 1) * 256], MV[l][:, mc, c * 128:(c + 1) * 128],
                             Pt[pi][mc][:, c * 256:(c + 1) * 256], start=(mc == 0), stop=(mc == 1))
                if MODE["half_den"]:
                    dn = ps[:, bd, 0:256].rearrange("p (g q) -> p g q", g=2)
                    for mc in range(2):
                        pv = Pt[pi][mc][:].rearrange("p (g q) -> p g q", g=4)
                        for h in range(2):
                            P.mm(dn, ones_h[h][:], pv[:, h::2, :], start=(mc == 0 and h == 0), stop=(mc == 1 and h == 1))
                else:
                    for mc in range(2):
                        P.mm(ps[:, bd, :], ones_m[1][:], Pt[pi][mc][:], start=(mc == 0), stop=(mc == 1))
                finish(u, bo, bd, 6)

            def attn_gen(s, pre_hook):
                ssl = slice(s * SUB, (s + 1) * SUB)
                units = []
                for bl in range(4):
                    blk = s * 4 + bl
                    units.append(dict(kind="swa", blk=blk, kv=0))
                    units.append(dict(kind="swa", blk=blk, kv=1))
                    units.append(dict(kind="mem", blk=blk))
                prev = None
                for i, u in enumerate(units):
                    u["pi"] = i % 2
                    (swa1 if u["kind"] == "swa" else mem1)(u)
                    pre_hook(i)
                    if prev is not None:
                        (swa2 if prev["kind"] == "swa" else mem2)(prev)
                    prev = u
                    yield
                (swa2 if prev["kind"] == "swa" else mem2)(prev)
                nb1 = norm_split([ybuf[:, 2 + c, :] for c in range(4)], G_GRP + l, 2, 512,
                                 [xn[:, 2 + c, ssl] for c in range(4)], SUB)
                nb2 = norm_split([ybuf[:, 6 + c, :] for c in range(2)], G_GRP + l, 6, 256,
                                 [xn[:, 6 + c, ssl] for c in range(2)], SUB)
                dgrp[s] = (lambda a, b: (lambda: (a(), b())))(nb1, nb2)

            nacc = [None] * NSUB
            pend = []
            wo = []

            def wout_gen(s):
                ssl = slice(s * SUB, (s + 1) * SUB)
                nacc[s] = NormAcc(s)
                kob = []
                if s == 1 and MODE["kouter_wo"]:
                    nko = MODE["kouter_wo"]
                    kob = [bank(lin_pool[0]) for _ in range(nko)]
                    for k in range(8):
                        for dc in range(nko):
                            P.mm(ps[:, kob[dc], :], wo[dc // 4][:, k, (dc % 4) * 128:(dc % 4 + 1) * 128], xn[:, k, ssl],
                                 start=(k == 0), stop=(k == 7))
                for dc in range(8):
                    w = wo[dc // 4]
                    if dc < len(kob):
                        b = kob[dc]
                    else:
                        b = bank(lin_pool[0])
                        for k in range(8):
                            P.mm(ps[:, b, :], w[:, k, (dc % 4) * 128:(dc % 4 + 1) * 128], xn[:, k, ssl],
                                 start=(k == 0), stop=(k == 7))
                    if len(pend) >= 2:
                        nacc[pend.pop(0)].flush()
                    P.tt(hT[:, dc, ssl], ps[:, b, :], hT[:, dc, ssl], ALU.add)
                    nacc[s].add(hT[:, dc, ssl], on_dve=(s == 0 and dc < MODE["sq_dve"]))
                    pend.append(s)
                    if s == 1 and MODE["early_a0"] and dc == MODE["early_a0"] - 1:
                        nacc[0].finish_a()
                    yield

            def step(g):
                try:
                    next(g)
                    return True
                except StopIteration:
                    return False

            def merge(ga, gb, na, nb, b_start, b_hold, after0=None):
                done_b = 0
                slots = na - b_start
                for i in range(na):
                    assert step(ga)
                    if i == 0 and after0 is not None:
                        after0()
                    if i >= b_start:
                        target = ((i - b_start + 1) * (nb - b_hold)) // slots
                        while done_b < target:
                            assert step(gb)
                            done_b += 1
                assert not step(ga)
                return done_b

            def drain(g):
                while step(g):
                    pass

            def hook_none(i):
                pass

            def hook_prev(i):
                if i == 1:
                    dgrp[0]()
                    dgrp[0] = None
                if i == MODE["dconv_at"]:
                    dconv[1]()
                    dconv[1] = None

            assert NSUB == 2
            if MODE["mixer"] == "seq":
                def hook_c1(i):
                    if i == 1:
                        dconv[1]()
                        dconv[1] = None

                def hook_g0(i):
                    if i == 1:
                        dgrp[0]()
                        dgrp[0] = None
                drain(proj_gen(0))
                drain(proj_gen(1))
                drain(attn_gen(0, hook_c1))
                drain(attn_gen(1, hook_g0))
                P.copy(kcar[l][:, :, :], kT[:, :, (NBLK - 1) * 128:NBLK * 128], eng=DVE)
                P.copy(vcar[l][:, :], Vt[:, NBLK - 1, :], eng=DVE)
                wo.append(WU.acquire(("out", l, 0)))
                wo.append(WU.acquire(("out", l, 1)))
                drain(wout_gen(0))
                dgrp[1]()
                dgrp[1] = None
                drain(wout_gen(1))
            else:
                nC, hC = MODE["c_split"]
                g0 = proj_gen(0)
                if late:
                    step(g0)
                    accs[1].finish(G_MIX + l, [xn[:, c, SUB:2 * SUB] for c in range(8)])
                drain(g0)
                lin_pool[0] = LIN
                P.tag = "mix-B"
                gbB = proj_gen(1)
                merge(attn_gen(0, hook_none), gbB, 12, 18, 0, MODE["b_hold"])
                wo.append(WU.acquire(("out", l, 0)))
                wo.append(WU.acquire(("out", l, 1)))
                P.tag = "mix-C"
                gb = wout_gen(0)
                merge(attn_gen(1, hook_prev), gb, 12, 8, 2, nC, after0=lambda: drain(gbB))
                P.tag = "mix-D"
                P.copy(kcar[l][:, :, :], kT[:, :, (NBLK - 1) * 128:NBLK * 128], eng=DVE)
                P.copy(vcar[l][:, :], Vt[:, NBLK - 1, :], eng=DVE)
                for _ in range(hC):
                    step(gb)
                dgrp[1]()
                dgrp[1] = None
                drain(gb)
                lin_pool[0] = None
                drain(wout_gen(1))
            WU.release(2)
            return nacc

        P.dma(identf[:], ident_d, eng=SP)
        x_dma(0, NXS)
        P.dma(Gt[:], gains_d, eng=SP)
        P.dma(CW[:], convw_d, eng=SP)
        P.dma(biasC[:], biasc_d, eng=POOL)
        P.dma(biasP[:], biasp_d, eng=POOL)
        if MODE["etab"]:
            for kv in range(2):
                P.act(biasC[:, kv, :], biasC[:, kv, :], AF.Exp, scale=0.125)
                P.act(biasP[:, kv, :], biasP[:, kv, :], AF.Exp, scale=0.125)
        P.copy(identb[:], identf[:], eng=DVE)
        for n in (1024, 512, 256, 1):
            P.memset(ones_m[n][:], 1.0 / n)
        P.memset(sel[:], 0.0)
        P.memset(sel[0:1, :], 1.0)
        P.memset(sel[32:33, :], 1.0)
        for i in range(2):
            P.memset(ones_h[i][:], 0.0)
            P.memset(ones_h[i][:, i * 64:(i + 1) * 64], 1.0)
            P.memset(sel_h[i][:], 0.0)
            P.memset(sel_h[i][0:1, i * 64:(i + 1) * 64], 1.0)
            P.memset(sel_h[i][32:33, i * 64:(i + 1) * 64], 1.0)
        for l in range(DEPTH):
            P.memset(MKz[l][:], 0.0)
            for j in range(2):
                P.memset(ccar[l][j][:], 0.0)
        P.memset(sk[:], 0.0)
        P.dma(sk[0:1, :], sinks_d, eng=SP)
        P.dma(sk[32:33, :], sinks_d, eng=SP)
        P.act(ske[:], sk[:], AF.Exp)
        P.copy(skh[:], ske[:])
        P.copy(skhf[:], skh[:])
        P.tt(skl[:], ske[:], skhf[:], ALU.subtract)
        P.memset(SR[:], 0.0)
        for lk in range(4):
            for par in range(2):
                src_hi = skh[0:1, lk * 4 + par:lk * 4 + 4:2].unsqueeze(2).to_broadcast([1, 2, 128])
                src_lo = skl[32:33, lk * 4 + par:lk * 4 + 4:2].unsqueeze(2).to_broadcast([1, 2, 128])
                P.copy(SR[0:1, lk, par * 256:(par + 1) * 256].rearrange("p (g q) -> p g q", g=2), src_hi)
                P.copy(SR[32:33, lk, par * 256:(par + 1) * 256].rearrange("p (g q) -> p g q", g=2), src_lo)
        if MODE["sink_bias"]:
            P.memset(RS[:], 0.0)
            for par in range(2):
                P.copy(RS[0:1, par, 0:8], skh[0:1, par:16:2])
                P.copy(RS[32:33, par, 0:8], skl[32:33, par:16:2])
            bsk = bank()
            for par in range(2):
                P.mm(ps[:, bsk, 0:128], sel_h[par][:], RS[:, par, :], start=(par == 0), stop=(par == 1))
            P.copy(SKB[:], ps[:, bsk, 0:128])
        for mc in range(2):
            P.dma(mstage_l[mc], mem_d[mc * 128:(mc + 1) * 128, :], eng=SP)
        for mc in range(2):
            for half in range(2):
                bk = bank()
                for cc in range(4):
                    c = half * 4 + cc
                    P.transpose(ps[:, bk, cc * 128:(cc + 1) * 128], mstage_l[mc][:, c * 128:(c + 1) * 128], identf[:])
                evac(memT[:, half * 4:half * 4 + 4, mc * 128:(mc + 1) * 128],
                     ps[:, bk, :].rearrange("p (c n) -> p c n", c=4))
        for l in range(depth):
            wm = WU.acquire(("mem", l))
            norm([memT[:, c, :] for c in range(8)], G_MEM + l, 0, 1024, [memn[:, c, :] for c in range(8)], MEM)
            for c in range(2):
                b = bank()
                for k in range(8):
                    P.mm(ps[:, b, 0:256], wm[:, k, c * 128:(c + 1) * 128], memn[:, k, :], start=(k == 0), stop=(k == 7))
                P.copy(MKz[l][0:64, 2 * c, :], ps[0:64, b, 0:256], eng=ACT)
                P.copy(MKz[l][64:128, 2 * c + 1, :], ps[64:128, b, 0:256], eng=DVE)
            for mc in range(2):
                b = bank()
                for k in range(8):
                    P.mm(ps[:, b, 0:256], memn[:, k, mc * 128:(mc + 1) * 128], wm[:, k, 256:512],
                         start=(k == 0), stop=(k == 7))
                evac(MV[l][:, mc, :], ps[:, b, 0:256])
            WU.release()

        done = False
        WD._pump()
        for t in range(ntiles):
            if MODE["load_pipe"]:
                accs = load_x_accs(t)
            else:
                load_x(t)
                accs = plain_accs()
            for l in range(depth):
                dst_xn = lambda c, ssl: xn[:, c, ssl]
                dst_ht = lambda c, ssl: hT[:, c, ssl]
                accs = ffn(1, l, accs, nxt=(G_MIX + l, dst_xn))
                if stop_after == ("ffn1", l):
                    done = True
                    break
                accs = mixer(l, t, accs)
                if stop_after == ("mixer", l):
                    done = True
                    break
                if l + 1 < depth:
                    nxt2 = (G_FFN1 + l + 1, dst_xn)
                elif MODE["store_inplace"] and not raw_out:
                    nxt2 = (G_FINAL, dst_ht)
                else:
                    nxt2 = None
                accs = ffn(2, l, accs, nxt=nxt2)
            if t + 1 < ntiles and not done:
                x_dma(t + 1, NXS)
            store_out(t, accs)
            if done:
                break

        P.emit(es)
        build_program.stats = P.stats
        build_program.prog = P
    return nc


def host_layout(inputs):
    f32 = np.float32
    g = lambda k: np.asarray(inputs[k], dtype=f32)
    gl = [g("g_ffn1")[0], g("g_ffn1")[1], g("g_mix")[0], g("g_mix")[1], g("g_mem")[0], g("g_mem")[1],
          g("g_grp")[0], g("g_grp")[1], g("g_ffn2")[0], g("g_ffn2")[1], g("g_final")]
    gains = np.ascontiguousarray(np.stack(gl, 0).reshape(11, 8, 128).transpose(2, 0, 1).reshape(128, 88))
    cw = g("conv_w")
    convw = np.ascontiguousarray(cw.reshape(DEPTH, 3, 2, 128).transpose(3, 0, 2, 1).reshape(128, 12))
    sinks = np.ascontiguousarray(g("sinks").reshape(1, 16))
    ident = np.eye(128, dtype=f32)
    slopes = np.array([2.0 ** (-8.0 * (i + 1) / 8) for i in range(8)], dtype=np.float64)
    s_i = np.arange(128)[:, None]
    q_i = np.arange(128)[None, :]
    bc = np.zeros((128, 2, 4, 128), f32)
    bp = np.zeros((128, 2, 4, 128), f32)
    for kv in range(2):
        for gg in range(4):
            sl = slopes[kv * 4 + gg]
            pos = (gg % 2) * 2 + gg // 2
            bc[:, kv, pos, :] = np.where(q_i >= s_i, -8.0 * sl * (q_i - s_i), MASKV)
            bp[:, kv, pos, :] = np.where(s_i > q_i, -8.0 * sl * (q_i + 128 - s_i), MASKV)
    shared = dict(
        w_ffn1_up=g("w_ffn1_up"), w_ffn1_down=g("w_ffn1_down"), w_in=g("w_in"), w_mem_kv=g("w_mem_kv"),
        w_out=g("w_out"), w_ffn2_up=g("w_ffn2_up"), w_ffn2_down=g("w_ffn2_down"),
        gains=gains, convw=convw, sinks=sinks, ident=ident,
        bias_cur=np.ascontiguousarray(bc.reshape(128, 2, 512)), bias_prev=np.ascontiguousarray(bp.reshape(128, 2, 512)),
    )
    x = g("x")
    mem = g("mem")
    in_maps = []
    for i in range(x.shape[0]):
        m = dict(shared)
        m["x"] = np.ascontiguousarray(x[i])
        m["mem"] = np.ascontiguousarray(mem[i])
        in_maps.append(m)
    return in_maps


_NC_CACHE = {}


def kernel(**inputs):
    in_maps = host_layout(inputs)
    if "nc" not in _NC_CACHE:
        _NC_CACHE["nc"] = build_program()
    nc = _NC_CACHE["nc"]
    res = run_bass_kernel_spmd(nc, in_maps, core_ids=list(range(len(in_maps))))
    out = np.stack([np.asarray(r["out"], dtype=np.float32) for r in res.results], axis=0)
    return out
```
